# Optimizing a Trainium2 kernel written in Bass

```python
import math
import jax, jax.numpy as jnp
from jax import lax
import numpy as np

D_MODEL = 1024
BATCH = 8
SEQ = 2048
DEPTH = 4
DEC_BATCH = 2
DEC_SEQ = 16384
PAST_LEN = 128

N_MIXERS = 2
N_MLA_LAYERS = (DEPTH + N_MIXERS - 1) // N_MIXERS
N_NA_LAYERS = DEPTH // N_MIXERS
RMS_EPS = 1e-6
GRID_W = 64
MLA_HEADS = 8
MLA_NOPE = 128
MLA_ROPE = 64
MLA_V = 128
MLA_Q_LORA = 384
MLA_KV_LORA = 256
ROPE_THETA = 10000.0
ATTN_Q_BLOCK = 128
NA_HEADS = 16
NA_HEAD_DIM = D_MODEL // NA_HEADS
NA_KH_MAX = 8
NA_KW = 16
NA_RPB_H = 2 * NA_KH_MAX - 1
NA_RPB_W = 2 * NA_KW - 1
PEER_HEADS = 8
PEER_NKEYS = 128
PEER_N_EXPERTS = PEER_NKEYS * PEER_NKEYS
PEER_QDIM = 256
PEER_TOPK = 16
PEER_CHUNK_MAX = 512

kernel_name = "mla_natten_peer_interleaved_encoder"


def rmsnorm(x, g):
    xf = x.astype(jnp.float32)
    y = xf * lax.rsqrt(jnp.mean(xf * xf, axis=-1, keepdims=True) + RMS_EPS)
    return (y * g.astype(jnp.float32)).astype(x.dtype)


def rope_tables(t_len):
    inv = ROPE_THETA ** (-jnp.arange(0, MLA_ROPE, 2, dtype=jnp.float32) / MLA_ROPE)
    ang = jnp.arange(t_len, dtype=jnp.float32)[:, None] * inv[None, :]
    return jnp.cos(ang), jnp.sin(ang)


def apply_rope(x, cos, sin):
    xf = x.astype(jnp.float32)
    x1, x2 = xf[..., : MLA_ROPE // 2], xf[..., MLA_ROPE // 2:]
    return jnp.concatenate([x1 * cos - x2 * sin, x1 * sin + x2 * cos], axis=-1).astype(x.dtype)


def mla(h, w_dq, q_norm, w_uq, w_dkv, kv_norm, w_ukv, w_o):
    B, T, _ = h.shape
    cq = rmsnorm(h @ w_dq, q_norm)
    q = (cq @ w_uq).reshape(B, T, MLA_HEADS, MLA_NOPE + MLA_ROPE)
    q_nope, q_pe = q[..., :MLA_NOPE], q[..., MLA_NOPE:]
    kv_a = h @ w_dkv
    c_kv = rmsnorm(kv_a[..., :MLA_KV_LORA], kv_norm)
    k_pe = kv_a[..., MLA_KV_LORA:]
    cos, sin = rope_tables(T)
    q_pe = apply_rope(q_pe, cos[:, None, :], sin[:, None, :])
    k_pe = apply_rope(k_pe, cos, sin)
    kv = (c_kv @ w_ukv).reshape(B, T, MLA_HEADS, MLA_NOPE + MLA_V)
    k_nope, v = kv[..., :MLA_NOPE], kv[..., MLA_NOPE:]
    nb = T // ATTN_Q_BLOCK
    scale = (MLA_NOPE + MLA_ROPE) ** -0.5

    def to_blocks(a):
        return a.reshape(B, nb, ATTN_Q_BLOCK, *a.shape[2:]).swapaxes(0, 1)

    def step(args):
        qn, qp = args
        s = (jnp.einsum('bqhd,bkhd->bhqk', qn, k_nope)
             + jnp.einsum('bqhr,bkr->bhqk', qp, k_pe))
        p = jax.nn.softmax(s.astype(jnp.float32) * scale, axis=-1).astype(v.dtype)
        return jnp.einsum('bhqk,bkhd->bqhd', p, v)

    o = lax.map(step, (to_blocks(q_nope), to_blocks(q_pe)))
    o = o.swapaxes(0, 1).reshape(B, T, MLA_HEADS * MLA_V)
    return o @ w_o


def na_indices(rows):
    t_len = rows * GRID_W
    kh = min(NA_KH_MAX, rows)
    t = jnp.arange(t_len, dtype=jnp.int32)
    r, c = t // GRID_W, t % GRID_W
    r0 = jnp.clip(r - kh // 2, 0, rows - kh)
    c0 = jnp.clip(c - NA_KW // 2, 0, GRID_W - NA_KW)
    kr = r0[:, None, None] + jnp.arange(kh, dtype=jnp.int32)[None, :, None]
    kc = c0[:, None, None] + jnp.arange(NA_KW, dtype=jnp.int32)[None, None, :]
    idx = (kr * GRID_W + kc).reshape(t_len, kh * NA_KW)
    rel = ((kr - r[:, None, None] + NA_KH_MAX - 1) * NA_RPB_W
           + (kc - c[:, None, None] + NA_KW - 1)).reshape(t_len, kh * NA_KW)
    return idx, rel


def neighbourhood_attention(h, w_qkv, b_qkv, rpb, w_o):
    B, T, D = h.shape
    rows = T // GRID_W
    qkv = (h @ w_qkv + b_qkv).reshape(B, T, 3, NA_HEADS, NA_HEAD_DIM)
    q = qkv[:, :, 0] * (NA_HEAD_DIM ** -0.5)
    k, v = qkv[:, :, 1], qkv[:, :, 2]
    idx, rel = na_indices(rows)
    rpb_flat = rpb.reshape(NA_HEADS, NA_RPB_H * NA_RPB_W)
    q_rows = q.reshape(B, rows, GRID_W, NA_HEADS, NA_HEAD_DIM).swapaxes(0, 1)

    def step(args):
        q_r, idx_r, rel_r = args
        kg = k[:, idx_r]
        vg = v[:, idx_r]
        s = (jnp.einsum('bqhd,bqkhd->bhqk', q_r, kg).astype(jnp.float32)
             + rpb_flat[:, rel_r].astype(jnp.float32)[None])
        p = jax.nn.softmax(s, axis=-1).astype(vg.dtype)
        return jnp.einsum('bhqk,bqkhd->bqhd', p, vg)

    o = lax.map(step, (q_rows, idx.reshape(rows, GRID_W, -1), rel.reshape(rows, GRID_W, -1)))
    o = o.swapaxes(0, 1).reshape(B, T, D)
    return o @ w_o


def peer(h, w_q, sub_keys, u_tab, v_tab):
    B, T, D = h.shape
    n = B * T
    chunk = math.gcd(n, PEER_CHUNK_MAX)
    xf = h.reshape(n // chunk, chunk, D)

    def step(xc):
        q = (xc @ w_q).reshape(chunk, PEER_HEADS, 2, PEER_QDIM // 2)
        s = jnp.einsum('chpd,hpkd->chpk', q, sub_keys).astype(jnp.float32)
        sv, si = lax.top_k(s, PEER_TOPK)
        cand = (sv[:, :, 0, :, None] + sv[:, :, 1, None, :]).reshape(chunk, PEER_HEADS, PEER_TOPK * PEER_TOPK)
        fv, fi = lax.top_k(cand, PEER_TOPK)
        ia = jnp.take_along_axis(si[:, :, 0], fi // PEER_TOPK, axis=-1)
        ib = jnp.take_along_axis(si[:, :, 1], fi % PEER_TOPK, axis=-1)
        expert = ia * PEER_NKEYS + ib
        g = jax.nn.softmax(fv, axis=-1)
        a = jnp.einsum('cd,chkd->chk', xc, u_tab[expert])
        w = (g * jax.nn.gelu(a.astype(jnp.float32), approximate=False)).astype(xc.dtype)
        return jnp.einsum('chk,chkd->cd', w, v_tab[expert])

    return lax.map(step, xf).reshape(B, T, D)


def trunk(x, norm_mix, norm_ffn, norm_final,
          mla_w_dq, mla_q_norm, mla_w_uq, mla_w_dkv, mla_kv_norm, mla_w_ukv, mla_w_o,
          na_w_qkv, na_b_qkv, na_rpb, na_w_o,
          peer_w_q, peer_sub_keys, peer_u, peer_v):
    for i in range(DEPTH):
        h = rmsnorm(x, norm_mix[i])
        j = i // N_MIXERS
        if i % N_MIXERS == 0:
            mix = mla(h, mla_w_dq[j], mla_q_norm[j], mla_w_uq[j], mla_w_dkv[j],
                      mla_kv_norm[j], mla_w_ukv[j], mla_w_o[j])
        else:
            mix = neighbourhood_attention(h, na_w_qkv[j], na_b_qkv[j], na_rpb[j], na_w_o[j])
        x = x + mix
        h = rmsnorm(x, norm_ffn[i])
        x = x + peer(h, peer_w_q[i], peer_sub_keys[i], peer_u[i], peer_v[i])
    return rmsnorm(x, norm_final)


def setup_inputs(seed: int = 0) -> dict:
    key = jax.random.key(seed)
    ks = jax.random.split(key, 24)
    f32 = jnp.float32

    def nrm(k, shape, scale):
        return jax.random.normal(k, shape, f32) * scale

    def gain(k, shape):
        return 1.0 + 0.02 * jax.random.normal(k, shape, f32)

    D = D_MODEL
    Lm, Ln = N_MLA_LAYERS, N_NA_LAYERS
    return {
        "x_prompt": nrm(ks[0], (BATCH, SEQ, D), 1.0),
        "x_sample": nrm(ks[1], (DEC_BATCH, DEC_SEQ, D), 1.0),
        "norm_mix": gain(ks[2], (DEPTH, D)),
        "norm_ffn": gain(ks[3], (DEPTH, D)),
        "norm_final": gain(ks[4], (D,)),
        "mla_w_dq": nrm(ks[5], (Lm, D, MLA_Q_LORA), D ** -0.5),
        "mla_q_norm": gain(ks[6], (Lm, MLA_Q_LORA)),
        "mla_w_uq": nrm(ks[7], (Lm, MLA_Q_LORA, MLA_HEADS * (MLA_NOPE + MLA_ROPE)), MLA_Q_LORA ** -0.5),
        "mla_w_dkv": nrm(ks[8], (Lm, D, MLA_KV_LORA + MLA_ROPE), D ** -0.5),
        "mla_kv_norm": gain(ks[9], (Lm, MLA_KV_LORA)),
        "mla_w_ukv": nrm(ks[10], (Lm, MLA_KV_LORA, MLA_HEADS * (MLA_NOPE + MLA_V)), MLA_KV_LORA ** -0.5),
        "mla_w_o": nrm(ks[11], (Lm, MLA_HEADS * MLA_V, D), (MLA_HEADS * MLA_V) ** -0.5),
        "na_w_qkv": nrm(ks[12], (Ln, D, 3 * D), D ** -0.5),
        "na_b_qkv": nrm(ks[13], (Ln, 3 * D), 0.02),
        "na_rpb": nrm(ks[14], (Ln, NA_HEADS, NA_RPB_H, NA_RPB_W), 0.1),
        "na_w_o": nrm(ks[15], (Ln, D, D), D ** -0.5),
        "peer_w_q": nrm(ks[16], (DEPTH, D, PEER_HEADS * PEER_QDIM), D ** -0.5),
        "peer_sub_keys": nrm(ks[17], (DEPTH, PEER_HEADS, 2, PEER_NKEYS, PEER_QDIM // 2), (PEER_QDIM // 2) ** -0.5),
        "peer_u": nrm(ks[18], (DEPTH, PEER_N_EXPERTS, D), D ** -0.5),
        "peer_v": nrm(ks[19], (DEPTH, PEER_N_EXPERTS, D), (PEER_HEADS * PEER_TOPK) ** -0.5),
    }


def reference(x_prompt, x_sample, norm_mix, norm_ffn, norm_final,
              mla_w_dq, mla_q_norm, mla_w_uq, mla_w_dkv, mla_kv_norm, mla_w_ukv, mla_w_o,
              na_w_qkv, na_b_qkv, na_rpb, na_w_o,
              peer_w_q, peer_sub_keys, peer_u, peer_v):
    y_prompt = trunk(x_prompt, norm_mix, norm_ffn, norm_final,
                     mla_w_dq, mla_q_norm, mla_w_uq, mla_w_dkv, mla_kv_norm, mla_w_ukv, mla_w_o,
                     na_w_qkv, na_b_qkv, na_rpb, na_w_o,
                     peer_w_q, peer_sub_keys, peer_u, peer_v)
    y_sample = trunk(x_sample, norm_mix, norm_ffn, norm_final,
                     mla_w_dq, mla_q_norm, mla_w_uq, mla_w_dkv, mla_kv_norm, mla_w_ukv, mla_w_o,
                     na_w_qkv, na_b_qkv, na_rpb, na_w_o,
                     peer_w_q, peer_sub_keys, peer_u, peer_v)
    return (y_prompt, y_sample)
```

```python
import contextlib
import numpy as np
import ml_dtypes
import concourse.bass as bass
import concourse.mybir as mybir
from concourse.bass_utils import run_bass_kernel_spmd

F32 = mybir.dt.float32
BF16 = mybir.dt.bfloat16
U32 = mybir.dt.uint32
I32 = mybir.dt.int32
AF = mybir.ActivationFunctionType
ALU = mybir.AluOpType
AX = mybir.AxisListType

D = 1024
NCH = 8
EPS = 1e-6
NCORES = 8


class Reg:
    __slots__ = ("name", "w", "re", "rd")

    def __init__(self, name=""):
        self.name = name
        self.w = None
        self.re = {}
        self.rd = []


class Sched:
    EPOCH = 30000

    def __init__(self, nc, es):
        self.nc = nc
        self.es = es
        self.engs = {"pe": nc.tensor, "act": nc.scalar, "dve": nc.vector, "pool": nc.gpsimd, "sp": nc.sync}
        self.cnt = {e: 0 for e in self.engs}
        self.esems = {e: [] for e in self.engs}
        self.known_e = {e: {} for e in self.engs}
        self.known_d = {e: {} for e in self.engs}
        self.dsem = {}
        self.nsem = 0

    def _newsem(self, name):
        self.nsem += 1
        return self.es.enter_context(self.nc.semaphore(name))

    def _esem(self, e, ep):
        while len(self.esems[e]) <= ep:
            self.esems[e].append(self._newsem(f"e_{e}_{len(self.esems[e])}"))
        return self.esems[e][ep]

    def _wait(self, eng, tok):
        if tok[0] == "e":
            _, e2, n = tok
            if self.known_e[eng].get(e2, 0) >= n:
                return
            ep, v = divmod(n - 1, self.EPOCH)
            self.engs[eng].wait_ge(self._esem(e2, ep), v + 1)
            self.known_e[eng][e2] = n
        else:
            _, name, v = tok
            if self.known_d[eng].get(name, 0) >= v:
                return
            self.engs[eng].wait_ge(self.dsem[name][0], v)
            self.known_d[eng][name] = v

    def _deps(self, eng, reads, writes):
        toks = []
        for r in reads:
            if r.w is not None:
                toks.append(r.w)
        for w in writes:
            if w.w is not None and not (eng == "pe" and w.w[0] == "e" and w.w[1] == "pe"):
                toks.append(w.w)
            for e2, n in w.re.items():
                toks.append(("e", e2, n))
            toks.extend(w.rd)
        for t in toks:
            self._wait(eng, t)

    def op(self, eng, reads, writes, fn):
        self._deps(eng, reads, writes)
        ins = fn(self.engs[eng])
        self.cnt[eng] += 1
        n = self.cnt[eng]
        ins.then_inc(self._esem(eng, (n - 1) // self.EPOCH), 1)
        tok = ("e", eng, n)
        for r in reads:
            if r.re.get(eng, 0) < n:
                r.re[eng] = n
        for w in writes:
            w.w = tok
            w.re = {}
            w.rd = []
        return tok

    def dma(self, q, sem, out, in_, reads, writes, **kw):
        if sem not in self.dsem:
            self.dsem[sem] = [self._newsem("d_" + sem), 0]
        self._deps(q, reads, writes)
        ins = self.engs[q].dma_start(out=out, in_=in_, **kw)
        s = self.dsem[sem]
        s[1] += 16
        ins.then_inc(s[0], 16)
        tok = ("d", sem, s[1])
        for r in reads:
            r.rd.append(tok)
        for w in writes:
            w.w = tok
            w.re = {}
            w.rd = []
        return tok

    def retoken(self, regs, tok):
        for r in regs:
            r.w = tok

    def barrier(self):
        for eng in self.engs:
            for e2 in self.engs:
                if e2 != eng and self.cnt[e2] > 0:
                    self._wait(eng, ("e", e2, self.cnt[e2]))
            for name, (sem, v) in self.dsem.items():
                if v > 0:
                    self._wait(eng, ("d", name, v))

    def finish(self, regs):
        for r in regs:
            if r.w is not None:
                self._wait("sp", r.w)


class K:
    pass


def build(cfg):
    TP, TS, DEPTH = cfg["TP"], cfg["TS"], cfg["DEPTH"]
    LAYERS = cfg.get("LAYERS", list(range(DEPTH)))
    DO_MIX = cfg.get("DO_MIX", True)
    DO_PEER = cfg.get("DO_PEER", True)
    DBG = cfg.get("DBG", False)
    NTOK = TP + TS
    assert NTOK % 512 == 0
    nc = bass.Bass("TRN2", target_bir_lowering=False)
    es = contextlib.ExitStack()
    S = Sched(nc, es)
    uid = [0]

    def dram(name, shape, dt, kind="Internal"):
        return nc.dram_tensor(name, list(shape), dt, kind=kind).ap()

    def sbx(stack, name, shape, dt):
        uid[0] += 1
        return stack.enter_context(nc.sbuf_tensor(f"{name}_{uid[0]}", list(shape), dt))

    def sb(name, shape, dt):
        return sbx(es, name, shape, dt)

    x_in = dram("x_in", [NTOK, D], F32, "ExternalInput")
    y_out = dram("y_out", [NTOK, D], F32, "ExternalOutput")
    ident_f = dram("ident_f", [128, 128], F32, "ExternalInput")
    gains = dram("gains", [128, 9, NCH], F32, "ExternalInput")
    iota_in = dram("iota_in", [128, 128], F32, "ExternalInput")
    xT = dram("xT", [D, NTOK], F32)
    xT_v = xT.rearrange("(c p) t -> p c t", p=128)
    dbg = {}
    if DBG:
        for nm, shp, dt in [("d_sv", [128, 256], F32), ("d_si", [128, 256], U32), ("d_fv", [128, 128], F32),
                            ("d_fi", [128, 128], U32), ("d_ia", [128, 128], F32), ("d_ib", [128, 128], F32),
                            ("d_g", [128, 128], F32), ("d_gs", [128, 128], F32)]:
            dbg[nm] = dram(nm, shp, dt, "ExternalOutput")
    r_dbg = Reg("dbg")

    identf = sb("identf", [128, 128], F32)
    identb = sb("identb", [128, 128], BF16)
    onesb = sb("onesb", [128, 128], BF16)
    gsb = sb("gsb", [128, 9, NCH], F32)
    iotaf = sb("iotaf", [128, 128], F32)
    epsb = sb("epsb", [128, 1], F32)
    r_const = Reg("const")
    S.dma("sp", "const", identf[:], ident_f[:, :], [], [r_const])
    S.dma("sp", "const", gsb[:], gains[:, :, :], [], [r_const])
    S.dma("sp", "const", iotaf[:], iota_in[:, :], [], [r_const])
    S.retoken([r_const], ("d", "const", S.dsem["const"][1]))
    S.op("dve", [r_const], [r_const], lambda e: e.tensor_copy(out=identb[:], in_=identf[:]))
    S.op("pool", [], [r_const], lambda e: e.memset(onesb[:], 1.0))
    S.op("pool", [], [r_const], lambda e: e.memset(epsb[:], EPS))

    psall = es.enter_context(nc.psum_tensor("psall", [128, 8, 512], F32))
    banks = [psall[:, i, :] for i in range(8)]
    rbank = [Reg(f"bank{i}") for i in range(8)]

    NB = NTOK // 512
    r_xT = [Reg(f"xT{b}") for b in range(NTOK // 256)]

    def xT_regs(t0, n):
        return [r_xT[i] for i in range(t0 // 256, (t0 + n + 255) // 256)]

    evac = [0]

    def evac_copy(src, dst, rs_, rd_, eng=None):
        if eng is None:
            eng = "act" if evac[0] % 2 == 0 else "dve"
            evac[0] += 1
        if eng == "act":
            return S.op("act", rs_, rd_, lambda e: e.copy(out=dst, in_=src))
        return S.op(eng, rs_, rd_, lambda e: e.tensor_copy(out=dst, in_=src))

    cast_rr = [0]

    def precast(name, n_outer, F):
        src = dram(name, [n_outer, 128, F], F32, "ExternalInput")
        dst = dram(name + "_b", [n_outer, 128, F], BF16)
        regs = [Reg(f"{name}{i}") for i in range(n_outer)]
        return src, dst, regs

    def emit_precast(src, dst, regs, stg, r_stg, stb, r_stb, F):
        CH = 4096
        for o in range(len(regs)):
            for f0 in range(0, F, CH):
                fw = min(CH, F - f0)
                k = cast_rr[0] % 3
                cast_rr[0] += 1
                S.dma("sp", f"cst_ld{k}", stg[k][:, :fw], src[o, :, f0:f0 + fw], [], [r_stg[k]])
                eng = ["pool", "act", "dve"][k]
                evac_copy(stg[k][:, :fw], stb[k][:, :fw], [r_stg[k]], [r_stb[k]], eng=eng)
                S.dma("pool", f"cst_st{k}", dst[o, :, f0:f0 + fw], stb[k][:, :fw], [r_stb[k]], [regs[o]])

    W = {}
    if DO_PEER:
        W["pu"] = precast("peer_u", DEPTH * 32, 4096)
        W["pv"] = precast("peer_v", DEPTH * 32, 4096)
        W["pwq"] = precast("peer_wq", DEPTH * 4, 4096)
        W["psk"] = precast("peer_sk", DEPTH, 2048)
    SCALE = (128 + 64) ** -0.5
    if DO_MIX and any(l % 2 == 0 for l in LAYERS):
        Lm = (DEPTH + 1) // 2
        W["wdq"] = precast("mla_wdq", Lm, 3072)
        W["wuq"] = precast("mla_wuq", Lm, 6144)
        W["wdkv"] = precast("mla_wdkv", Lm, 4096)
        W["wukv"] = precast("mla_wukv", Lm, 4096)
        W["wo"] = precast("mla_wo", Lm, 8192)
        mla_g = dram("mla_g", [Lm, 128, 5], F32, "ExternalInput")
        rope_c = dram("rope_c", [128, NTOK], F32, "ExternalInput")
        rope_s = dram("rope_s", [128, NTOK], F32, "ExternalInput")
        qn_d = dram("qn_d", [8, 128, NTOK], BF16)
        qp_d = dram("qp_d", [4, 128, NTOK], BF16)
        oT_d = dram("oT_d", [8, 128, NTOK], BF16)
        latP = dram("latP", [384, TP], BF16)
        CW = min(TS, 1024)
        NCHK = TS // CW
        latS_k = [dram(f"latS{k}", [384, CW], BF16) for k in range(NCHK)]
        latA_k = [dram(f"latA{k}", [4 * 384, CW], BF16) for k in range(NCHK)]
        r_qn = [Reg() for _ in range(NTOK // 512)]
        r_oT = [Reg() for _ in range(NTOK // 512)]
        r_latP = Reg()
        r_latS = Reg()
        r_latA = Reg()


    if DO_MIX and any(l % 2 == 1 for l in LAYERS):
        Ln = DEPTH // 2
        W["nqkv"] = precast("na_wqkv", Ln * 6, 4096)
        W["nwo"] = precast("na_wo", Ln, 8192)
        na_b = dram("na_b", [Ln, 128, 16], F32, "ExternalInput")
        na_bv = dram("na_bv", [Ln, 128, D], F32, "ExternalInput")
        na_BI = dram("na_BI", [Ln, 128, 4096], F32, "ExternalInput")
        na_ED = dram("na_ED", [Ln, 4, 8, 128, 768], F32, "ExternalInput")
        na_RM = dram("na_RM", [128, 96], F32, "ExternalInput")
        na_MK = dram("na_MK", [128, 8], F32, "ExternalInput")
        RP, RS_ = TP // 64, TS // 64
        EXTP, EXTS = (RP + 8) * 64, (RS_ + 8) * 64
        naq = dram("naq", [8, 128, NTOK], BF16)
        nakE = [dram("nakEp", [8, 128, EXTP], BF16), dram("nakEs", [8, 128, EXTS], BF16)]
        navE = [dram("navEp", [EXTP, D], BF16), dram("navEs", [EXTS, D], BF16)]
        sendK_h = [dram(f"sendK{g}", [512, 512], BF16) for g in range(2)]
        sendV_h = [dram(f"sendV{g}", [256, D], BF16) for g in range(2)]
        recvK_h = [dram(f"recvK{g}", [4 * 512, 512], BF16) for g in range(2)]
        recvV_h = [dram(f"recvV{g}", [4 * 256, D], BF16) for g in range(2)]
        r_naq = Reg(); r_nak = [Reg(), Reg()]; r_nav = [Reg(), Reg()]
        r_send = Reg(); r_recv = Reg()
    with contextlib.ExitStack() as ps:
        stg = [sbx(ps, "stg", [128, 4096], F32) for _ in range(3)]
        stb = [sbx(ps, "stb", [128, 4096], BF16) for _ in range(3)]
        r_stg = [Reg() for _ in range(3)]
        r_stb = [Reg() for _ in range(3)]
        for key, (src, dst, regs) in W.items():
            Fk = src.shape[2]
            if key in ("pu", "pv", "pwq", "psk"):
                sel = [o for o in range(len(regs))
                       if (o // (len(regs) // DEPTH)) in LAYERS]
                for o in sel:
                    emit_precast(src[o:o + 1], dst[o:o + 1], regs[o:o + 1], stg, r_stg, stb, r_stb, Fk)
            else:
                emit_precast(src, dst, regs, stg, r_stg, stb, r_stb, Fk)
        S.barrier()

    with contextlib.ExitStack() as ps:
        xin = [sbx(ps, "xin", [128, 4, D], F32) for i in range(2)]
        r_xin = [Reg() for i in range(2)]
        xtb = [sbx(ps, "xtb", [128, NCH, 512], F32) for i in range(2)]
        r_xtb = [Reg() for i in range(2)]
        x_in_v = x_in.rearrange("(b j p) d -> b p j d", j=4, p=128)
        for b in range(NB):
            s = b % 2
            S.dma("sp", f"xin{s}", xin[s][:], x_in_v[b], [], [r_xin[s]])
            for j in range(4):
                for half in range(2):
                    bk = (2 * j + half) % 4
                    for cc in range(4):
                        c = half * 4 + cc
                        S.op("pe", [r_xin[s], r_const], [rbank[bk]],
                             lambda e, c=c, j=j, cc=cc, bk=bk, s=s: e.transpose(
                                 out=banks[bk][:, cc * 128:(cc + 1) * 128],
                                 in_=xin[s][:, j, c * 128:(c + 1) * 128], identity=identf[:]))
                    evac_copy(banks[bk][:, :].rearrange("p (c t) -> p c t", c=4),
                              xtb[s][:, half * 4:(half + 1) * 4, j * 128:(j + 1) * 128], [rbank[bk]], [r_xtb[s]])
            S.dma("pool", f"xtbst{s}", xT_v[:, :, b * 512:(b + 1) * 512], xtb[s][:], [r_xtb[s]],
                  xT_regs(b * 512, 512))
        S.barrier()

    def rmsnorm_fm(src, r_src, nchunk, T, gcol, dst, r_dst, sq, r_sq, rs, r_rs, bk, nfeat=None):
        nfeat = nfeat or nchunk * 128
        S.op("act", r_src, [r_sq], lambda e: e.activation(out=sq, in_=src, func=AF.Square))
        for c in range(nchunk):
            S.op("pe", [r_sq, r_const], [rbank[bk]],
                 lambda e, c=c: e.matmul(banks[bk][:, :T], lhsT=onesb[:], rhs=sq[:, c, :],
                                         start=(c == 0), stop=(c == nchunk - 1)))
        S.op("act", [rbank[bk], r_const], [r_rs],
             lambda e: e.activation(out=rs, in_=banks[bk][:, :T], func=AF.Sqrt, scale=1.0 / nfeat,
                                    bias=epsb[:, 0:1]))
        S.op("dve", [r_rs], [r_rs], lambda e: e.reciprocal(out=rs, in_=rs))
        for c in range(nchunk):
            S.op("dve", r_src + [r_rs, r_const], [r_dst],
                 lambda e, c=c: e.scalar_tensor_tensor(out=dst[:, c, :], in0=src[:, c, :], scalar=gcol(c),
                                                       in1=rs, op0=ALU.mult, op1=ALU.mult))

    def peer_phase(layer):
        TB = 256
        nblk = NTOK // TB
        if cfg.get("PEER_BLOCKS") is not None:
            nblk = cfg["PEER_BLOCKS"]
        NGRP = cfg.get("PEER_NGRP", 32)
        _, pu_b, r_pu = W["pu"]
        _, pv_b, r_pv = W["pv"]
        _, pwq_b, r_pwq = W["pwq"]
        _, psk_b, r_psk = W["psk"]
        with contextlib.ExitStack() as ps:
            A = lambda name, shape, dt: sbx(ps, name, shape, dt)
            wq = A("wq", [128, NCH, 2048], BF16)
            r_wq = Reg()
            skT = A("skT", [128, 16, 128], BF16)
            r_sk = Reg()
            for gq in range(4):
                S.dma("sp", "pw_ld", wq[:, :, gq * 512:(gq + 1) * 512],
                      pwq_b[layer * 4 + gq].rearrange("p (c n) -> p c n", c=NCH), [r_pwq[layer * 4 + gq]], [r_wq])
            S.dma("sp", "pw_ld", skT[:], psk_b[layer].rearrange("p (g k) -> p g k", g=16), [r_psk[layer]], [r_sk])
            tokf = ("d", "pw_ld", S.dsem["pw_ld"][1])
            S.retoken([r_wq, r_sk], tokf)
            xb = A("xb", [128, NCH, TB], F32); r_xb = Reg()
            hb = A("hb", [128, NCH, TB], BF16); r_hb = Reg()
            sq = A("sq", [128, NCH, TB], BF16); r_sq = Reg()
            rs = A("rs", [128, TB], F32); r_rs = Reg()
            qT = A("qT", [128, 16, TB], BF16); r_qT = Reg()
            sc = A("sc", [128, 2048], F32); r_sc = Reg()
            work = A("work", [128, 256], F32); r_work = Reg()
            sv = A("sv", [128, 16, 16], F32); r_sv = Reg()
            si = A("si", [128, 16, 16], U32); r_si = Reg()
            sif = A("sif", [128, 16, 16], F32); r_sif = Reg()
            cand = A("cand", [128, 8, 256], F32); r_cand = Reg()
            fv = A("fv", [128, 8, 16], F32); r_fv = Reg()
            fi = A("fi", [128, 8, 16], U32); r_fi = Reg()
            ab_u = A("ab_u", [128, 2, 128], U32); r_abu = Reg()
            ab_f = A("ab_f", [128, 2, 128], F32); r_abf = Reg()
            eq = A("eq", [128, 128, 16], F32); r_eq = Reg()
            iab = A("iab", [128, 3, 128], F32); r_iab = Reg()
            gtmp = A("gtmp", [128, 8, 16], F32); r_gtmp = Reg()
            gsum = A("gsum", [128, 8], F32); r_gsum = Reg()
            slT = A("slT", [128, 3, TB], F32); r_slT = Reg()
            NE = 4
            E1 = [A("E1", [128, 128], BF16) for _ in range(NE)]; r_E1 = [Reg() for _ in range(NE)]
            E0 = [A("E0", [128, 128], BF16) for _ in range(NE)]; r_E0 = [Reg() for _ in range(NE)]
            Gs = A("Gs", [128, TB, 128], BF16); r_Gs = Reg()
            NU = 2
            Ut = [A("Ut", [128, NCH, 512], BF16) for _ in range(NU)]; r_Ut = [Reg() for _ in range(NU)]
            Vt = [A("Vt", [128, 4, D], BF16) for _ in range(NU)]; r_Vt = [Reg() for _ in range(NU)]
            gel = [A("gel", [128, TB], F32) for _ in range(2)]; r_gel = [Reg() for _ in range(2)]
            wT = [A("wT", [128, TB], BF16) for _ in range(3)]; r_wT = [Reg() for _ in range(3)]
            osb = A("osb", [128, D], F32); r_osb = Reg()
            iota16 = iotaf[:, 0:16]
            ecount = 0
            for blk in range(nblk):
                t0 = blk * TB
                S.dma("sp", "pxb_ld", xb[:], xT_v[:, :, t0:t0 + TB], xT_regs(t0, TB), [r_xb])
                rmsnorm_fm(xb[:], [r_xb], NCH, TB, lambda c: gsb[:, 4 + layer, c:c + 1], hb[:], r_hb,
                           sq[:], r_sq, rs[:], r_rs, 0)
                for g in range(16):
                    bk = (g // 2) % 2
                    off = (g % 2) * TB
                    for c in range(NCH):
                        S.op("pe", [r_wq, r_hb], [rbank[bk]],
                             lambda e, g=g, c=c, bk=bk, off=off: e.matmul(
                                 banks[bk][:, off:off + TB], lhsT=wq[:, c, g * 128:(g + 1) * 128], rhs=hb[:, c, :],
                                 start=(c == 0), stop=(c == NCH - 1)))
                    if g % 2 == 1:
                        evac_copy(banks[bk][:, :].rearrange("p (g t) -> p g t", g=2), qT[:, g - 1:g + 1, :],
                                  [rbank[bk]], [r_qT])
                for tt in range(TB // 128):
                    ts_ = slice(tt * 128, (tt + 1) * 128)
                    for g in range(16):
                        bk = 4 + g // 4
                        S.op("pe", [r_qT, r_sk], [rbank[bk]],
                             lambda e, g=g, bk=bk, ts_=ts_: e.matmul(
                                 banks[bk][:, (g % 4) * 128:(g % 4 + 1) * 128], lhsT=qT[:, g, ts_], rhs=skT[:, g, :],
                                 start=True, stop=True))
                        if g % 4 == 3:
                            evac_copy(banks[bk][:, :], sc[:, (g // 4) * 512:(g // 4 + 1) * 512], [rbank[bk]], [r_sc],
                                      eng="act")
                    for g in range(16):
                        sg = sc[:, g * 128:(g + 1) * 128]
                        S.op("dve", [r_sc], [r_sv], lambda e, g=g, sg=sg: e.max(out=sv[:, g, 0:8], in_=sg))
                        S.op("dve", [r_sc, r_sv], [r_si],
                             lambda e, g=g, sg=sg: e.max_index(out=si[:, g, 0:8], in_max=sv[:, g, 0:8], in_values=sg))
                        S.op("dve", [r_sc, r_sv], [r_work],
                             lambda e, g=g, sg=sg: e.match_replace(out=work[:, 0:128], in_to_replace=sv[:, g, 0:8],
                                                                   in_values=sg, imm_value=-1e30))
                        S.op("dve", [r_work], [r_sv], lambda e, g=g: e.max(out=sv[:, g, 8:16], in_=work[:, 0:128]))
                        S.op("dve", [r_work, r_sv], [r_si],
                             lambda e, g=g: e.max_index(out=si[:, g, 8:16], in_max=sv[:, g, 8:16],
                                                        in_values=work[:, 0:128]))
                    svv = sv[:].rearrange("q (h p) k -> q h p k", p=2)
                    c4 = cand[:].rearrange("q h (a b) -> q h a b", a=16)
                    S.op("dve", [r_sv], [r_cand], lambda e, svv=svv, c4=c4: e.tensor_tensor(
                        out=c4, in0=svv[:, :, 0, :].unsqueeze(3).to_broadcast([128, 8, 16, 16]),
                        in1=svv[:, :, 1, :].unsqueeze(2).to_broadcast([128, 8, 16, 16]), op=ALU.add))
                    for h in range(8):
                        ch = cand[:, h, :]
                        S.op("dve", [r_cand], [r_fv], lambda e, h=h, ch=ch: e.max(out=fv[:, h, 0:8], in_=ch))
                        S.op("dve", [r_cand, r_fv], [r_fi],
                             lambda e, h=h, ch=ch: e.max_index(out=fi[:, h, 0:8], in_max=fv[:, h, 0:8], in_values=ch))
                        S.op("dve", [r_cand, r_fv], [r_work],
                             lambda e, h=h, ch=ch: e.match_replace(out=work[:], in_to_replace=fv[:, h, 0:8],
                                                                   in_values=ch, imm_value=-1e30))
                        S.op("dve", [r_work], [r_fv], lambda e, h=h: e.max(out=fv[:, h, 8:16], in_=work[:]))
                        S.op("dve", [r_work, r_fv], [r_fi],
                             lambda e, h=h: e.max_index(out=fi[:, h, 8:16], in_max=fv[:, h, 8:16], in_values=work[:]))
                    fi2 = fi[:].rearrange("q h k -> q (h k)")
                    S.op("dve", [r_fi], [r_abu], lambda e, fi2=fi2: e.tensor_single_scalar(
                        out=ab_u[:, 0, :], in_=fi2, scalar=4, op=ALU.logical_shift_right))
                    S.op("dve", [r_fi], [r_abu], lambda e, fi2=fi2: e.tensor_single_scalar(
                        out=ab_u[:, 1, :], in_=fi2, scalar=15, op=ALU.bitwise_and))
                    S.op("dve", [r_abu], [r_abf], lambda e: e.tensor_copy(out=ab_f[:], in_=ab_u[:]))
                    S.op("dve", [r_si], [r_sif], lambda e: e.tensor_copy(out=sif[:], in_=si[:]))
                    sifv = sif[:].rearrange("q (h p) k -> q h p k", p=2)
                    eq4 = eq[:].rearrange("q (h k) a -> q h k a", h=8)
                    for p_ in range(2):
                        abv = ab_f[:, p_, :].rearrange("q (h k) -> q h k", h=8)
                        S.op("dve", [r_abf, r_const], [r_eq], lambda e, abv=abv: e.tensor_tensor(
                            out=eq4, in0=abv.unsqueeze(3).to_broadcast([128, 8, 16, 16]),
                            in1=iota16.unsqueeze(1).unsqueeze(1).to_broadcast([128, 8, 16, 16]), op=ALU.is_equal))
                        S.op("dve", [r_eq, r_sif], [r_eq], lambda e, p_=p_: e.tensor_tensor(
                            out=eq4, in0=eq4, in1=sifv[:, :, p_, :].unsqueeze(2).to_broadcast([128, 8, 16, 16]),
                            op=ALU.mult))
                        S.op("dve", [r_eq], [r_iab], lambda e, p_=p_: e.tensor_reduce(
                            out=iab[:, p_, :], in_=eq[:], axis=AX.X, op=ALU.add))
                    S.op("dve", [r_fv], [r_gtmp], lambda e: e.tensor_tensor(
                        out=gtmp[:], in0=fv[:], in1=fv[:, :, 0:1].to_broadcast([128, 8, 16]), op=ALU.subtract))
                    S.op("act", [r_gtmp], [r_gtmp], lambda e: e.activation(out=gtmp[:], in_=gtmp[:], func=AF.Exp))
                    S.op("dve", [r_gtmp], [r_gsum], lambda e: e.tensor_reduce(
                        out=gsum[:], in_=gtmp[:], axis=AX.X, op=ALU.add))
                    S.op("dve", [r_gsum], [r_gsum], lambda e: e.reciprocal(out=gsum[:], in_=gsum[:]))
                    S.op("dve", [r_gtmp, r_gsum], [r_iab], lambda e: e.tensor_tensor(
                        out=iab[:, 2, :].rearrange("q (h k) -> q h k", h=8), in0=gtmp[:],
                        in1=gsum[:].unsqueeze(2).to_broadcast([128, 8, 16]), op=ALU.mult))
                    if DBG and blk == 0 and tt == 0:
                        S.dma("pool", "dbg", dbg["d_sv"][:, :], sv[:].rearrange("q g k -> q (g k)"), [r_sv], [r_dbg])
                        S.dma("pool", "dbg", dbg["d_si"][:, :], si[:].rearrange("q g k -> q (g k)"), [r_si], [r_dbg])
                        S.dma("pool", "dbg", dbg["d_fv"][:, :], fv[:].rearrange("q g k -> q (g k)"), [r_fv], [r_dbg])
                        S.dma("pool", "dbg", dbg["d_fi"][:, :], fi[:].rearrange("q g k -> q (g k)"), [r_fi], [r_dbg])
                        S.dma("pool", "dbg", dbg["d_ia"][:, :], iab[:, 0, :], [r_iab], [r_dbg])
                        S.dma("pool", "dbg", dbg["d_ib"][:, :], iab[:, 1, :], [r_iab], [r_dbg])
                        S.dma("pool", "dbg", dbg["d_g"][:, :], iab[:, 2, :], [r_iab], [r_dbg])
                    for q3 in range(3):
                        S.op("pe", [r_iab, r_const], [rbank[2]], lambda e, q3=q3: e.transpose(
                            out=banks[2][:, q3 * 128:(q3 + 1) * 128], in_=iab[:, q3, :], identity=identf[:]))
                    evac_copy(banks[2][:, 0:384].rearrange("p (q t) -> p q t", q=3), slT[:, :, ts_], [rbank[2]], [r_slT],
                              eng="act")
                    for tl in range(128):
                        t = tt * 128 + tl
                        k = ecount % NE
                        ecount += 1
                        S.op("dve", [r_slT, r_const], [r_E1[k]], lambda e, k=k, t=t: e.tensor_scalar(
                            out=E1[k][:], in0=iotaf[:], scalar1=slT[:, 1, t:t + 1], scalar2=None, op0=ALU.is_equal))
                        S.op("dve", [r_slT, r_const], [r_E0[k]], lambda e, k=k, t=t: e.tensor_scalar(
                            out=E0[k][:], in0=iotaf[:], scalar1=slT[:, 0, t:t + 1], scalar2=slT[:, 2, t:t + 1],
                            op0=ALU.is_equal, op1=ALU.mult))
                        bk = 2 + (tl // 4) % 2
                        S.op("pe", [r_E1[k], r_E0[k]], [rbank[bk]], lambda e, k=k, bk=bk, tl=tl: e.matmul(
                            banks[bk][:, (tl % 4) * 128:(tl % 4 + 1) * 128], lhsT=E1[k][:], rhs=E0[k][:],
                            start=True, stop=True))
                        if tl % 4 == 3:
                            evac_copy(banks[bk][:, :].rearrange("p (t i) -> p t i", t=4), Gs[:, t - 3:t + 1, :],
                                      [rbank[bk]], [r_Gs], eng="act")
                if DBG and blk == 0:
                    S.dma("pool", "dbg", dbg["d_gs"][:, :], gsb[:, 0, 0:1].to_broadcast([128, 128]), [r_const], [r_dbg]) if False else None
                nexp = NGRP * 4
                for grp in range(NGRP):
                    us = grp % NU
                    o = layer * 32 + grp
                    S.dma("sp", f"pu_ld{us}", Ut[us][:], pu_b[o].rearrange("p (c e) -> p c e", c=NCH),
                          [r_pu[o]], [r_Ut[us]])
                    S.dma("sp", f"pv_ld{us}", Vt[us][:], pv_b[o].rearrange("p (i d) -> p i d", i=4),
                          [r_pv[o]], [r_Vt[us]])
                    for il in range(4):
                        i = grp * 4 + il
                        ba = i % 2
                        for c in range(NCH):
                            S.op("pe", [r_Ut[us], r_hb], [rbank[ba]], lambda e, c=c, il=il, us=us, ba=ba: e.matmul(
                                banks[ba][:, :TB], lhsT=Ut[us][:, c, il * 128:(il + 1) * 128], rhs=hb[:, c, :],
                                start=(c == 0), stop=(c == NCH - 1)))
                        gk = i % 2
                        S.op("act", [rbank[ba]], [r_gel[gk]], lambda e, gk=gk, ba=ba: e.activation(
                            out=gel[gk][:], in_=banks[ba][:, :TB], func=AF.Gelu))
                        wk = i % 3
                        S.op("dve", [r_gel[gk], r_Gs], [r_wT[wk]], lambda e, gk=gk, wk=wk, i=i: e.tensor_tensor(
                            out=wT[wk][:], in0=gel[gk][:], in1=Gs[:, :, i], op=ALU.mult))
                        for tt in range(TB // 128):
                            for half in range(2):
                                bk = 4 + tt * 2 + half
                                S.op("pe", [r_wT[wk], r_Vt[us]], [rbank[bk]],
                                     lambda e, wk=wk, us=us, il=il, tt=tt, half=half, bk=bk, i=i: e.matmul(
                                         banks[bk][:, :], lhsT=wT[wk][:, tt * 128:(tt + 1) * 128],
                                         rhs=Vt[us][:, il, half * 512:(half + 1) * 512],
                                         start=(i == 0), stop=(i == nexp - 1)))
                for tt in range(TB // 128):
                    for half in range(2):
                        bk = 4 + tt * 2 + half
                        evac_copy(banks[bk][:, :], osb[:, half * 512:(half + 1) * 512], [rbank[bk]], [r_osb])
                    for half in range(2):
                        bk = 2 + half
                        for cc in range(4):
                            c = half * 4 + cc
                            S.op("pe", [r_osb, r_const], [rbank[bk]], lambda e, c=c, cc=cc, bk=bk: e.transpose(
                                out=banks[bk][:, cc * 128:(cc + 1) * 128], in_=osb[:, c * 128:(c + 1) * 128],
                                identity=identf[:]))
                        S.op("dve", [rbank[bk], r_xb], [r_xb], lambda e, half=half, bk=bk, tt=tt: e.tensor_tensor(
                            out=xb[:, half * 4:(half + 1) * 4, tt * 128:(tt + 1) * 128],
                            in0=xb[:, half * 4:(half + 1) * 4, tt * 128:(tt + 1) * 128],
                            in1=banks[bk][:, :].rearrange("p (c t) -> p c t", c=4), op=ALU.add))
                S.dma("pool", "pxb_st", xT_v[:, :, t0:t0 + TB], xb[:], [r_xb], xT_regs(t0, TB))
            S.barrier()


    def mla_phase(layer):
        j = layer // 2
        _, wdq_b, r_wdq = W["wdq"]
        _, wuq_b, r_wuq = W["wuq"]
        _, wdkv_b, r_wdkv = W["wdkv"]
        _, wukv_b, r_wukv = W["wukv"]
        _, wo_b, r_wo = W["wo"]
        TBK = 512
        with contextlib.ExitStack() as ps:
            A = lambda name, shape, dt: sbx(ps, name, shape, dt)
            wdq = A("wdq", [128, NCH, 384], BF16)
            wuq = A("wuq", [128, 3, 2048], BF16)
            wdkv = A("wdkv", [128, NCH, 512], BF16)
            mg = A("mg", [128, 5], F32)
            r_w = Reg()
            S.dma("sp", "mw_ld", wdq[:], wdq_b[j].rearrange("p (c n) -> p c n", c=NCH), [r_wdq[j]], [r_w])
            S.dma("sp", "mw_ld", wuq[:], wuq_b[j].rearrange("p (c n) -> p c n", c=3), [r_wuq[j]], [r_w])
            S.dma("sp", "mw_ld", wdkv[:], wdkv_b[j].rearrange("p (c n) -> p c n", c=NCH), [r_wdkv[j]], [r_w])
            S.dma("sp", "mw_ld", mg[:], mla_g[j], [], [r_w])
            S.retoken([r_w], ("d", "mw_ld", S.dsem["mw_ld"][1]))
            xb = A("xb", [128, NCH, TBK], F32); r_xb = Reg()
            hb = A("hb", [128, NCH, TBK], BF16); r_hb = Reg()
            sq = A("sq", [128, NCH, TBK], BF16); r_sq = Reg()
            rs = A("rs", [128, TBK], F32); r_rs = Reg()
            cqn = A("cqn", [128, 3, TBK], BF16); r_cqn = Reg()
            cs = A("cs", [128, 2, TBK], F32); r_cs = Reg()
            qst = [A("qst", [128, 8, TBK], BF16) for _ in range(2)]; r_qst = [Reg() for _ in range(2)]
            pst = [A("pst", [128, 4, TBK], BF16) for _ in range(2)]; r_pst = [Reg() for _ in range(2)]
            lst = [A("lst", [128, 3, TBK], BF16) for _ in range(2)]; r_lst = [Reg() for _ in range(2)]
            t1 = A("t1", [128, TBK], F32); r_t1 = Reg()
            t2 = A("t2", [128, TBK], F32); r_t2 = Reg()
            for blk in range(NTOK // TBK):
                t0 = blk * TBK
                s2 = blk % 2
                S.dma("sp", "mxb_ld", xb[:], xT_v[:, :, t0:t0 + TBK], xT_regs(t0, TBK), [r_xb])
                S.dma("sp", "mcs_ld", cs[:, 0, :], rope_c[:, t0:t0 + TBK], [], [r_cs])
                S.dma("sp", "mcs_ld", cs[:, 1, :], rope_s[:, t0:t0 + TBK], [], [r_cs])
                S.retoken([r_cs], ("d", "mcs_ld", S.dsem["mcs_ld"][1]))
                rmsnorm_fm(xb[:], [r_xb], NCH, TBK, lambda c: gsb[:, layer, c:c + 1], hb[:], r_hb,
                           sq[:], r_sq, rs[:], r_rs, 0)
                for cc in range(3):
                    for c in range(NCH):
                        S.op("pe", [r_w, r_hb], [rbank[1 + cc]], lambda e, cc=cc, c=c: e.matmul(
                            banks[1 + cc][:, :], lhsT=wdq[:, c, cc * 128:(cc + 1) * 128], rhs=hb[:, c, :],
                            start=(c == 0), stop=(c == NCH - 1)))
                rmsnorm_fm(psall[:, 1:4, :], [rbank[1], rbank[2], rbank[3]], 3, TBK, lambda c: mg[:, c:c + 1],
                           cqn[:], r_cqn, sq[:, 0:3, :], r_sq, rs[:], r_rs, 0)
                for h in range(8):
                    bk = 4 + h % 2
                    for c in range(3):
                        S.op("pe", [r_w, r_cqn], [rbank[bk]], lambda e, h=h, c=c, bk=bk: e.matmul(
                            banks[bk][:, :], lhsT=wuq[:, c, h * 128:(h + 1) * 128], rhs=cqn[:, c, :],
                            start=(c == 0), stop=(c == 2)))
                    evac_copy(banks[bk][:, :], qst[s2][:, h, :], [rbank[bk]], [r_qst[s2]])
                S.dma("pool", f"mq_st{s2}", qn_d[:, :, t0:t0 + TBK].rearrange("h p t -> p h t"), qst[s2][:],
                      [r_qst[s2]], [r_qn[blk]])
                for jp in range(4):
                    for c in range(3):
                        S.op("pe", [r_w, r_cqn], [rbank[6]], lambda e, jp=jp, c=c: e.matmul(
                            banks[6][:, :], lhsT=wuq[:, c, 1024 + jp * 128:1024 + (jp + 1) * 128], rhs=cqn[:, c, :],
                            start=(c == 0), stop=(c == 2)))
                    for c in range(3):
                        S.op("pe", [r_w, r_cqn], [rbank[7]], lambda e, jp=jp, c=c: e.matmul(
                            banks[7][:, :], lhsT=wuq[:, c, 1536 + jp * 128:1536 + (jp + 1) * 128], rhs=cqn[:, c, :],
                            start=(c == 0), stop=(c == 2)))
                    S.op("dve", [rbank[6], r_cs], [r_t1], lambda e: e.tensor_tensor(
                        out=t1[:], in0=banks[6][:, :], in1=cs[:, 0, :], op=ALU.mult))
                    S.op("dve", [rbank[7], r_cs], [r_t2], lambda e: e.tensor_tensor(
                        out=t2[:], in0=banks[7][:, :], in1=cs[:, 1, :], op=ALU.mult))
                    S.op("pool", [r_t1, r_t2], [r_pst[s2]], lambda e, jp=jp: e.tensor_tensor(
                        out=pst[s2][:, jp, :], in0=t1[:], in1=t2[:], op=ALU.add))
                S.dma("pool", f"mp_st{s2}", qp_d[:, :, t0:t0 + TBK].rearrange("h p t -> p h t"), pst[s2][:],
                      [r_pst[s2]], [r_qn[blk]])
                for cc in range(4):
                    for c in range(NCH):
                        S.op("pe", [r_w, r_hb], [rbank[1 + cc]], lambda e, cc=cc, c=c: e.matmul(
                            banks[1 + cc][:, :], lhsT=wdkv[:, c, cc * 128:(cc + 1) * 128], rhs=hb[:, c, :],
                            start=(c == 0), stop=(c == NCH - 1)))
                rmsnorm_fm(psall[:, 1:3, :], [rbank[1], rbank[2]], 2, TBK, lambda c: mg[:, 3 + c:4 + c],
                           lst[s2][:, 0:2, :], r_lst[s2], sq[:, 0:2, :], r_sq, rs[:], r_rs, 0)
                S.op("dve", [rbank[3], r_cs], [r_t1], lambda e: e.tensor_tensor(
                    out=t1[:], in0=banks[3][:, :], in1=cs[:, 0, :], op=ALU.mult))
                S.op("dve", [rbank[4], r_cs], [r_t2], lambda e: e.tensor_tensor(
                    out=t2[:], in0=banks[4][:, :], in1=cs[:, 1, :], op=ALU.mult))
                S.op("pool", [r_t1, r_t2], [r_lst[s2]], lambda e: e.tensor_tensor(
                    out=lst[s2][:, 2, :], in0=t1[:], in1=t2[:], op=ALU.add))
                if t0 < TP:
                    dst = latP[:, t0:t0 + TBK].rearrange("(c p) t -> p c t", p=128)
                    S.dma("pool", f"ml_st{s2}", dst, lst[s2][:], [r_lst[s2]], [r_latP])
                else:
                    lt0 = t0 - TP
                    dst = latS_k[lt0 // CW][:, lt0 % CW:lt0 % CW + TBK].rearrange("(c p) t -> p c t", p=128)
                    S.dma("pool", f"ml_st{s2}", dst, lst[s2][:], [r_lst[s2]], [r_latS])
            S.barrier()
        if cfg.get("USE_CC", True):
            for k in range(NCHK):
                cc_allgather(latS_k[k][:, :], latA_k[k][:, :], r_latS, r_latA)
            S.barrier()
        with contextlib.ExitStack() as ps:
            A = lambda name, shape, dt: sbx(ps, name, shape, dt)
            TKMAX = max(TP, 4 * TS)
            wukv = A("wukv", [128, 2, 2048], BF16); r_wk = Reg()
            S.dma("sp", "mw_ld", wukv[:], wukv_b[j].rearrange("p (c n) -> p c n", c=2), [r_wukv[j]], [r_wk])
            kpe2 = A("kpe2", [128, TKMAX], BF16); r_kpe = Reg()
            KT = A("KT", [128, TKMAX], BF16); r_KT = Reg()
            Vh = A("Vh", [128, TKMAX // 128, 128], BF16); r_Vh = Reg()
            ckb = [A("ckb", [128, 2, 512], BF16) for _ in range(2)]; r_ckb = [Reg() for _ in range(2)]
            Qn = [A("Qn", [128, 512], BF16) for _ in range(2)]; r_Qn = [Reg() for _ in range(2)]
            Qp = [A("Qp", [128, 512], BF16) for _ in range(2)]; r_Qp = [Reg() for _ in range(2)]
            pT = [A("pT", [128, 512], BF16) for _ in range(3)]; r_pT = [Reg() for _ in range(3)]
            rden = A("rden", [128, 512], F32); r_rden = Reg()
            ost = [A("ost", [128, 512], BF16) for _ in range(2)]; r_ost = [Reg() for _ in range(2)]
            nq = 0
            ncb = 0
            for seg in range(2):
                if seg == 0:
                    Tq, Tk, q0 = TP, TP, 0
                    lat_rows = lambda r0, r1, k0, k1: latP[r0:r1, k0:k1]
                    r_lat = r_latP
                else:
                    Tq, Tk, q0 = TS, 4 * TS, TP
                    def lat_rows(r0, r1, k0, k1):
                        rk = k0 // TS
                        w0 = k0 - rk * TS
                        kk, off = w0 // CW, w0 % CW
                        assert off + (k1 - k0) <= CW
                        return latA_k[kk][rk * 384 + r0:rk * 384 + r1, off:off + (k1 - k0)]
                    r_lat = r_latA
                w_ = CW if seg else TP
                for rk in range(Tk // w_):
                    S.dma("sp", "mk_ld", kpe2[:, rk * w_:(rk + 1) * w_], lat_rows(256, 384, rk * w_, (rk + 1) * w_),
                          [r_lat], [r_kpe])
                S.retoken([r_kpe], ("d", "mk_ld", S.dsem["mk_ld"][1]))
                for h in range(8):
                    for kb in range(Tk // 512):
                        cs_ = ncb % 2
                        ncb += 1
                        S.dma("sp", f"mc_ld{cs_}", ckb[cs_][:],
                              lat_rows(0, 256, kb * 512, (kb + 1) * 512).rearrange("(c p) t -> p c t", p=128),
                              [r_lat], [r_ckb[cs_]])
                        for c in range(2):
                            S.op("pe", [r_wk, r_ckb[cs_]], [rbank[0]], lambda e, c=c, cs_=cs_, h=h: e.matmul(
                                banks[0][:, :], lhsT=wukv[:, c, h * 256:h * 256 + 128], rhs=ckb[cs_][:, c, :],
                                start=(c == 0), stop=(c == 1)))
                        evac_copy(banks[0][:, :], KT[:, kb * 512:(kb + 1) * 512], [rbank[0]], [r_KT])
                        for kt in range(4):
                            for c in range(2):
                                S.op("pe", [r_wk, r_ckb[cs_]], [rbank[1]], lambda e, c=c, cs_=cs_, h=h, kt=kt: e.matmul(
                                    banks[1][:, kt * 128:(kt + 1) * 128], lhsT=ckb[cs_][:, c, kt * 128:(kt + 1) * 128],
                                    rhs=wukv[:, c, h * 256 + 128:h * 256 + 256], start=(c == 0), stop=(c == 1)))
                        evac_copy(banks[1][:, :].rearrange("p (k d) -> p k d", k=4), Vh[:, kb * 4:(kb + 1) * 4, :],
                                  [rbank[1]], [r_Vh])
                    hp = (h % 2) * 64
                    for qb in range(Tq // 512):
                        qs = nq % 2
                        nq += 1
                        tq0 = q0 + qb * 512
                        S.dma("sp", f"mqn_ld{qs}", Qn[qs][:], qn_d[h, :, tq0:tq0 + 512], [r_qn[tq0 // 512]], [r_Qn[qs]])
                        S.dma("sp", f"mqp_ld{qs}", Qp[qs][:], qp_d[h // 2, :, tq0:tq0 + 512], [r_qn[tq0 // 512]],
                              [r_Qp[qs]])
                        nkt = Tk // 128

                        def s_mm(kt):
                            bs = 2 + kt % 3
                            S.op("pe", [r_KT, r_Qn[qs]], [rbank[bs]], lambda e: e.matmul(
                                banks[bs][:, :], lhsT=KT[:, kt * 128:(kt + 1) * 128], rhs=Qn[qs][:],
                                start=True, stop=False))
                            S.op("pe", [r_kpe, r_Qp[qs]], [rbank[bs]], lambda e: e.matmul(
                                banks[bs][:, :], lhsT=kpe2[hp:hp + 64, kt * 128:(kt + 1) * 128],
                                rhs=Qp[qs][hp:hp + 64, :], start=False, stop=True))
                            S.op("act", [rbank[bs]], [r_pT[kt % 3]], lambda e: e.activation(
                                out=pT[kt % 3][:], in_=banks[bs][:, :], func=AF.Exp, scale=SCALE))

                        def pv_mm(kt):
                            S.op("pe", [r_Vh, r_pT[kt % 3]], [rbank[5]], lambda e: e.matmul(
                                banks[5][:, :], lhsT=Vh[:, kt, :], rhs=pT[kt % 3][:],
                                start=(kt == 0), stop=(kt == nkt - 1)))
                            S.op("pe", [r_const, r_pT[kt % 3]], [rbank[6]], lambda e: e.matmul(
                                banks[6][:, :], lhsT=onesb[:], rhs=pT[kt % 3][:],
                                start=(kt == 0), stop=(kt == nkt - 1)))

                        s_mm(0)
                        if nkt > 1:
                            s_mm(1)
                        for kt in range(nkt):
                            pv_mm(kt)
                            if kt + 2 < nkt:
                                s_mm(kt + 2)
                        os_ = nq % 2
                        S.op("dve", [rbank[6]], [r_rden], lambda e: e.reciprocal(out=rden[:], in_=banks[6][:, :]))
                        S.op("dve", [rbank[5], r_rden], [r_ost[os_]], lambda e: e.tensor_tensor(
                            out=ost[os_][:], in0=banks[5][:, :], in1=rden[:], op=ALU.mult))
                        S.dma("pool", f"mo_st{os_}", oT_d[h, :, tq0:tq0 + 512], ost[os_][:], [r_ost[os_]],
                              [r_oT[tq0 // 512]])
            S.barrier()
        with contextlib.ExitStack() as ps:
            A = lambda name, shape, dt: sbx(ps, name, shape, dt)
            wo = A("wo", [128, 8, D], BF16); r_w = Reg()
            S.dma("sp", "mw_ld", wo[:], wo_b[j].rearrange("p (h n) -> p h n", h=8), [r_wo[j]], [r_w])
            xb = [A("xb", [128, NCH, TBK], F32) for _ in range(2)]; r_xb = [Reg() for _ in range(2)]
            ob = [A("ob", [128, 8, TBK], BF16) for _ in range(2)]; r_ob = [Reg() for _ in range(2)]
            for blk in range(NTOK // TBK):
                t0 = blk * TBK
                s2 = blk % 2
                S.dma("sp", f"oxb_ld{s2}", xb[s2][:], xT_v[:, :, t0:t0 + TBK], xT_regs(t0, TBK), [r_xb[s2]])
                S.dma("sp", f"oob_ld{s2}", ob[s2][:], oT_d[:, :, t0:t0 + TBK].rearrange("h p t -> p h t"),
                      [r_oT[blk]], [r_ob[s2]])
                for c in range(NCH):
                    bk = c % 4
                    for h in range(8):
                        S.op("pe", [r_w, r_ob[s2]], [rbank[bk]], lambda e, c=c, h=h, bk=bk: e.matmul(
                            banks[bk][:, :], lhsT=wo[:, h, c * 128:(c + 1) * 128], rhs=ob[s2][:, h, :],
                            start=(h == 0), stop=(h == 7)))
                    S.op("dve", [rbank[bk], r_xb[s2]], [r_xb[s2]], lambda e, c=c, bk=bk: e.tensor_tensor(
                        out=xb[s2][:, c, :], in0=xb[s2][:, c, :], in1=banks[bk][:, :], op=ALU.add))
                S.dma("pool", f"oxb_st{s2}", xT_v[:, :, t0:t0 + TBK], xb[s2][:], [r_xb[s2]], xT_regs(t0, TBK))
            S.barrier()


    def cc_allgather(src, dst, r_src, r_dst):
        S._deps("pool", [r_src], [r_dst])
        ins = nc.gpsimd.collective_compute("AllGather", ALU.bypass, replica_groups=[[0, 1, 2, 3], [4, 5, 6, 7]],
                                           ins=[src], outs=[dst])
        if "cc" not in S.dsem:
            S.dsem["cc"] = [S._newsem("d_cc"), 0]
        S.dsem["cc"][1] += 1
        ins.then_inc(S.dsem["cc"][0], 1)
        r_dst.w = ("d", "cc", S.dsem["cc"][1])
        r_dst.re = {}
        r_dst.rd = []
        r_src.rd.append(r_dst.w)

    def na_phase(layer):
        j = layer // 2
        _, nqkv_b, r_nqkv = W["nqkv"]
        _, nwo_b, r_nwo = W["nwo"]
        TBK = 512
        with contextlib.ExitStack() as ps:
            A = lambda name, shape, dt: sbx(ps, name, shape, dt)
            wqkv = A("wqkv", [128, NCH, 3072], BF16); r_w = Reg()
            for s6 in range(6):
                S.dma("sp", "nw_ld", wqkv[:, :, s6 * 512:(s6 + 1) * 512],
                      nqkv_b[j * 6 + s6].rearrange("p (c n) -> p c n", c=NCH), [r_nqkv[j * 6 + s6]], [r_w])
            bqk = A("bqk", [128, 16], F32)
            bvb = A("bvb", [128, D], F32)
            zt = A("zt", [128, 2048], BF16)
            S.dma("sp", "nw_ld", bqk[:], na_b[j], [], [r_w])
            S.dma("sp", "nw_ld", bvb[:], na_bv[j], [], [r_w])
            S.retoken([r_w], ("d", "nw_ld", S.dsem["nw_ld"][1]))
            r_zt = Reg()
            S.op("pool", [], [r_zt], lambda e: e.memset(zt[:], 0.0))
            for (lo, hi) in [(0, 256), (256 + TP, 512 + TP)]:
                S.dma("pool", "nz_st", nakE[0][:, :, lo:hi].rearrange("h p t -> p h t"),
                      zt[:].rearrange("p (h t) -> p h t", h=8), [r_zt], [r_nak[0]])
                S.dma("pool", "nz_st", navE[0][lo:hi, :].rearrange("(j p) d -> p j d", p=128),
                      zt[:].rearrange("p (j d) -> p j d", j=2), [r_zt], [r_nav[0]])
            xb = A("xb", [128, NCH, TBK], F32); r_xb = Reg()
            hb = A("hb", [128, NCH, TBK], BF16); r_hb = Reg()
            sq = A("sq", [128, NCH, TBK], BF16); r_sq = Reg()
            rs = A("rs", [128, TBK], F32); r_rs = Reg()
            qst = [A("qst", [128, 8, TBK], BF16) for _ in range(2)]; r_qst = [Reg() for _ in range(2)]
            kst = [A("kst", [128, 8, TBK], BF16) for _ in range(2)]; r_kst = [Reg() for _ in range(2)]
            vst = [A("vst", [128, 4, D], BF16) for _ in range(2)]; r_vst = [Reg() for _ in range(2)]
            for blk in range(NTOK // TBK):
                t0 = blk * TBK
                s2 = blk % 2
                seg = 0 if t0 < TP else 1
                lt0 = t0 - (0 if seg == 0 else TP)
                S.dma("sp", "nxb_ld", xb[:], xT_v[:, :, t0:t0 + TBK], xT_regs(t0, TBK), [r_xb])
                rmsnorm_fm(xb[:], [r_xb], NCH, TBK, lambda c: gsb[:, layer, c:c + 1], hb[:], r_hb,
                           sq[:], r_sq, rs[:], r_rs, 0)
                for kind in range(2):
                    st, r_st = (qst, r_qst) if kind == 0 else (kst, r_kst)
                    for hp in range(8):
                        bk = 1 + hp % 2
                        for c in range(NCH):
                            S.op("pe", [r_w, r_hb], [rbank[bk]], lambda e, c=c: e.matmul(
                                banks[bk][:, :], lhsT=wqkv[:, c, kind * 1024 + hp * 128:kind * 1024 + (hp + 1) * 128],
                                rhs=hb[:, c, :], start=(c == 0), stop=(c == NCH - 1)))
                        S.op("dve", [rbank[bk], r_w], [r_st[s2]], lambda e: e.tensor_scalar(
                            out=st[s2][:, hp, :], in0=banks[bk][:, :], scalar1=bqk[:, kind * 8 + hp:kind * 8 + hp + 1],
                            scalar2=(0.125 if kind == 0 else 1.0), op0=ALU.add, op1=ALU.mult))
                S.dma("pool", f"nq_st{s2}", naq[:, :, t0:t0 + TBK].rearrange("h p t -> p h t"), qst[s2][:],
                      [r_qst[s2]], [r_naq])
                S.dma("pool", f"nk_st{s2}", nakE[seg][:, :, 256 + lt0:256 + lt0 + TBK].rearrange("h p t -> p h t"),
                      kst[s2][:], [r_kst[s2]], [r_nak[seg]])
                for tl in range(4):
                    for half in range(2):
                        bk = 3 + half
                        for c in range(NCH):
                            S.op("pe", [r_w, r_hb], [rbank[bk]], lambda e, c=c: e.matmul(
                                banks[bk][:, :], lhsT=hb[:, c, tl * 128:(tl + 1) * 128],
                                rhs=wqkv[:, c, 2048 + half * 512:2048 + (half + 1) * 512],
                                start=(c == 0), stop=(c == NCH - 1)))
                        S.op("dve", [rbank[bk], r_w], [r_vst[s2]], lambda e: e.tensor_tensor(
                            out=vst[s2][:, tl, half * 512:(half + 1) * 512], in0=banks[bk][:, :],
                            in1=bvb[:, half * 512:(half + 1) * 512], op=ALU.add))
                S.dma("pool", f"nv_st{s2}", navE[seg][256 + lt0:256 + lt0 + TBK, :].rearrange("(j p) d -> p j d", p=128),
                      vst[s2][:], [r_vst[s2]], [r_nav[seg]])
            S.barrier()
        if cfg.get("NA_STOP", 9) <= 1:
            return
        with contextlib.ExitStack() as ps:
            A = lambda name, shape, dt: sbx(ps, name, shape, dt)
            mk = A("mk", [128, 8], F32); r_mk = Reg()
            S.dma("sp", "nh_ld", mk[:], na_MK[:, :], [], [r_mk])
            for g in range(2):
                sK = sendK_h[g].rearrange("(h p) t -> h p t", p=128)
                S.dma("sp", "nh_snd", sK[:, :, 0:256], nakE[1][g * 4:(g + 1) * 4, :, 256:512], [r_nak[1]], [r_send])
                S.dma("sp", "nh_snd", sK[:, :, 256:512], nakE[1][g * 4:(g + 1) * 4, :, TS:TS + 256], [r_nak[1]], [r_send])
            S.dma("sp", "nh_snd", sendV_h[0][:, :], navE[1][256:512, :], [r_nav[1]], [r_send])
            S.dma("sp", "nh_snd", sendV_h[1][:, :], navE[1][TS:TS + 256, :], [r_nav[1]], [r_send])
            S.retoken([r_send], ("d", "nh_snd", S.dsem["nh_snd"][1]))
            S.barrier()
            for g in range(2):
                cc_allgather(sendK_h[g][:, :], recvK_h[g][:, :], r_send, r_recv)
                cc_allgather(sendV_h[g][:, :], recvV_h[g][:, :], r_send, r_recv)
            S.barrier()
            rk = A("rk", [128, 4, 8, 512], BF16); r_rk = Reg()
            rv = A("rv", [128, 4, 4, D], BF16); r_rv = Reg()
            for g in range(2):
                rKv = recvK_h[g].rearrange("(r h p) t -> r p h t", r=4, p=128)
                rVv = recvV_h[g].rearrange("(r j p) d -> r p j d", r=4, p=128)
                for r in range(4):
                    S.dma("sp", "nh_ld", rk[:, r, g * 4:(g + 1) * 4, :], rKv[r], [r_recv], [r_rk])
                    S.dma("sp", "nh_ld", rv[:, r, g * 2:(g + 1) * 2, :], rVv[r], [r_recv], [r_rv])
            S.retoken([r_rk, r_rv, r_mk], ("d", "nh_ld", S.dsem["nh_ld"][1]))
            hk = A("hk", [128, 2, 8, 256], BF16); r_hk = Reg()
            hv = A("hv", [128, 2, 2, D], BF16); r_hv = Reg()
            for side in range(2):
                tsl = slice(256, 512) if side == 0 else slice(0, 256)
                jsl = slice(2, 4) if side == 0 else slice(0, 2)
                for r in range(4):
                    m = mk[:, side * 4 + r:side * 4 + r + 1]
                    if r == 0:
                        S.op("dve", [r_rk, r_mk], [r_hk], lambda e: e.tensor_scalar(
                            out=hk[:, side], in0=rk[:, r, :, tsl], scalar1=m, scalar2=None, op0=ALU.mult))
                        S.op("dve", [r_rv, r_mk], [r_hv], lambda e: e.tensor_scalar(
                            out=hv[:, side], in0=rv[:, r, jsl, :], scalar1=m, scalar2=None, op0=ALU.mult))
                    else:
                        S.op("dve", [r_rk, r_mk, r_hk], [r_hk], lambda e: e.scalar_tensor_tensor(
                            out=hk[:, side], in0=rk[:, r, :, tsl], scalar=m, in1=hk[:, side], op0=ALU.mult, op1=ALU.add))
                        S.op("dve", [r_rv, r_mk, r_hv], [r_hv], lambda e: e.scalar_tensor_tensor(
                            out=hv[:, side], in0=rv[:, r, jsl, :], scalar=m, in1=hv[:, side], op0=ALU.mult, op1=ALU.add))
                lo = 0 if side == 0 else 256 + TS
                S.dma("pool", "nh_st", nakE[1][:, :, lo:lo + 256].rearrange("h p t -> p h t"), hk[:, side],
                      [r_hk], [r_nak[1]])
                S.dma("pool", "nh_st", navE[1][lo:lo + 256, :].rearrange("(j p) d -> p j d", p=128), hv[:, side],
                      [r_hv], [r_nav[1]])
            S.barrier()
        if cfg.get("NA_STOP", 9) <= 2:
            return
        with contextlib.ExitStack() as ps:
            A = lambda name, shape, dt: sbx(ps, name, shape, dt)
            wo = A("nwo", [128, 8, D], BF16); r_w = Reg()
            S.dma("sp", "nw_ld", wo[:], nwo_b[j].rearrange("p (h n) -> p h n", h=8), [r_nwo[j]], [r_w])
            BI = A("BI", [128, 8, 512], F32)
            S.dma("sp", "nw_ld", BI[:], na_BI[j].rearrange("p (h n) -> p h n", h=8), [], [r_w])
            RM = A("RM", [128, 96], F32)
            S.dma("sp", "nw_ld", RM[:], na_RM[:, :], [], [r_w])
            S.retoken([r_w], ("d", "nw_ld", S.dsem["nw_ld"][1]))
            RB = min(16, RP, RS_)
            EB = RB + 8
            NBF = 1
            KTb = [A("KTb", [128, 8, EB * 64], BF16) for _ in range(NBF)]; r_KTb = [Reg() for _ in range(NBF)]
            QTb = [A("QTb", [128, 8, RB * 64], BF16) for _ in range(2)]; r_QTb = [Reg() for _ in range(2)]
            for hh in range(2):
                S.op("pool", [], [r_QTb[hh]], lambda e: e.memset(QTb[hh][:], 0.0))
            VE = [A("VE", [128, EB // 2, D], BF16) for _ in range(NBF)]; r_VE = [Reg() for _ in range(NBF)]
            VO = [A("VO", [128, EB // 2 - 1, D], BF16) for _ in range(NBF)]; r_VO = [Reg() for _ in range(NBF)]
            OTb = A("OTb", [128, 8, RB * 64], BF16); r_OTb = Reg()
            S.op("pool", [], [r_OTb], lambda e: e.memset(OTb[:], 0.0))
            xb = A("nxb", [128, NCH, RB * 64], F32); r_xb = Reg()
            EDt = [A("EDt", [128, 768], F32) for _ in range(2)]; r_EDt = [Reg() for _ in range(2)]
            sbias = [A("sbias", [128, 768], F32) for _ in range(2)]; r_sb = [Reg() for _ in range(2)]
            pT = [A("npT", [128, 6, 128], BF16) for _ in range(2)]; r_pT = [Reg() for _ in range(2)]
            rden = [A("nrden", [128, 128], F32) for _ in range(2)]; r_rden = [Reg() for _ in range(2)]
            unit = 0
            nblk_all = 0
            ned = 0
            for seg in range(2):
                R = RP if seg == 0 else RS_
                tok0 = 0 if seg == 0 else TP
                for rb in range(R // RB):
                    r_b = rb * RB
                    bs_ = nblk_all % NBF
                    nblk_all += 1
                    e0 = r_b * 64
                    S.dma("sp", f"nkt_ld{bs_}", KTb[bs_][:], nakE[seg][:, :, e0:e0 + EB * 64].rearrange("h p t -> p h t"),
                          [r_nak[seg]], [r_KTb[bs_]])
                    for hh in range(2):
                        S.dma("sp", f"nqt_ld{hh}", QTb[hh][hh * 64:(hh + 1) * 64],
                              naq[:, hh * 64:(hh + 1) * 64, tok0 + r_b * 64:tok0 + (r_b + RB) * 64].rearrange("h p t -> p h t"),
                              [r_naq], [r_QTb[hh]])
                    S.dma("sp", f"nve_ld{bs_}", VE[bs_][:],
                          navE[seg][e0:e0 + EB * 64, :].rearrange("(j p) d -> p j d", p=128), [r_nav[seg]], [r_VE[bs_]])
                    S.dma("sp", f"nvo_ld{bs_}", VO[bs_][:],
                          navE[seg][e0 + 64:e0 + 64 + (EB // 2 - 1) * 128, :].rearrange("(j p) d -> p j d", p=128),
                          [r_nav[seg]], [r_VO[bs_]])
                    S.dma("sp", "nxb2_ld", xb[:], xT_v[:, :, tok0 + r_b * 64:tok0 + (r_b + RB) * 64],
                          xT_regs(tok0 + r_b * 64, RB * 64), [r_xb])
                    for lr in range(RB):
                        r = r_b + lr
                        if cfg.get("NA_ATT", 2) == 0 or (cfg.get("NA_ATT", 2) == 1 and (r < 4 or r >= R - 4)):
                            continue
                        if r < 4:
                            cls, rr, nch, cstart = 1, r, 6, 0
                        elif r >= R - 4:
                            cls, rr, nch, cstart = 2, r - (R - 4), 6, (R - 4) - r_b
                        else:
                            cls, rr, nch, cstart = 0, 0, 4, lr
                        for hp in range(8):
                            u = unit % 2
                            unit += 1
                            bS = [u * 2, u * 2 + 1]
                            bO = 4 + u
                            bD = 6 + u
                            for i in range(nch):
                                for hh in range(2):
                                    col = (i % 4) * 128 + hh * 64
                                    bk = bS[i // 4]
                                    ks = (cstart + 2 * i) * 64
                                    S.op("pe", [r_KTb[bs_], r_QTb[hh]], [rbank[bk]], lambda e: e.matmul(
                                        banks[bk][:, col:col + 64], lhsT=KTb[bs_][:, hp, ks:ks + 128],
                                        rhs=QTb[hh][:, hp, lr * 64:(lr + 1) * 64],
                                        start=True, stop=True))
                            if cls == 0:
                                S.op("dve", [rbank[bS[0]], r_w], [r_sb[u]], lambda e: e.tensor_tensor(
                                    out=sbias[u][:, 0:512], in0=banks[bS[0]][:, :], in1=BI[:, hp, :], op=ALU.add))
                                S.op("act", [r_sb[u]], [r_pT[u]], lambda e: e.activation(
                                    out=pT[u][:, 0:4, :], in_=sbias[u][:, 0:512].rearrange("p (c n) -> p c n", c=4),
                                    func=AF.Exp))
                            else:
                                ed = ned % 2
                                ned += 1
                                S.dma("sp", f"ned_ld{ed}", EDt[ed][:], na_ED[j, rr, hp], [], [r_EDt[ed]])
                                for i in range(6):
                                    mcol = ((seg * 2 + (cls - 1)) * 4 + rr) * 6 + i
                                    bk = bS[i // 4]
                                    cs0 = (i % 4) * 128
                                    S.op("dve", [rbank[bk], r_EDt[ed], r_w], [r_sb[u]], lambda e: e.scalar_tensor_tensor(
                                        out=sbias[u][:, i * 128:(i + 1) * 128], in0=banks[bk][:, cs0:cs0 + 128],
                                        scalar=RM[:, mcol:mcol + 1], in1=EDt[ed][:, i * 128:(i + 1) * 128],
                                        op0=ALU.add, op1=ALU.add))
                                S.op("act", [r_sb[u]], [r_pT[u]], lambda e: e.activation(
                                    out=pT[u][:, :, :], in_=sbias[u][:, :].rearrange("p (c n) -> p c n", c=6),
                                    func=AF.Exp))
                            for i in range(nch):
                                ce = cstart + 2 * i
                                if ce % 2 == 0:
                                    vt, r_vt, vi = VE[bs_], r_VE[bs_], ce // 2
                                else:
                                    vt, r_vt, vi = VO[bs_], r_VO[bs_], (ce - 1) // 2
                                S.op("pe", [r_vt, r_pT[u]], [rbank[bO]], lambda e: e.matmul(
                                    banks[bO][:, 0:128], lhsT=vt[:, vi, hp * 128:(hp + 1) * 128], rhs=pT[u][:, i, :],
                                    start=(i == 0), stop=(i == nch - 1)))
                                S.op("pe", [r_const, r_pT[u]], [rbank[bD]], lambda e: e.matmul(
                                    banks[bD][:, 0:128], lhsT=onesb[:], rhs=pT[u][:, i, :],
                                    start=(i == 0), stop=(i == nch - 1)))
                            S.op("dve", [rbank[bD]], [r_rden[u]], lambda e: e.reciprocal(
                                out=rden[u][:], in_=banks[bD][:, 0:128]))
                            S.op("dve", [rbank[bO], r_rden[u]], [r_rden[u]], lambda e: e.tensor_tensor(
                                out=rden[u][:], in0=banks[bO][:, 0:128], in1=rden[u][:], op=ALU.mult))
                            for hh in range(2):
                                psl = slice(hh * 64, (hh + 1) * 64)
                                S.op("act", [r_rden[u]], [r_OTb], lambda e: e.copy(
                                    out=OTb[psl, hp, lr * 64:(lr + 1) * 64], in_=rden[u][psl, hh * 64:(hh + 1) * 64]))
                    for n0 in range(0, RB * 64, 512):
                        if cfg.get("NA_SKIPPROJ"):
                            break
                        nw = min(512, RB * 64 - n0)
                        for c in range(NCH):
                            bk = c % 2
                            for hp in range(8):
                                S.op("pe", [r_w, r_OTb], [rbank[bk]], lambda e: e.matmul(
                                    banks[bk][:, :nw], lhsT=wo[:, hp, c * 128:(c + 1) * 128], rhs=OTb[:, hp, n0:n0 + nw],
                                    start=(hp == 0), stop=(hp == 7)))
                            S.op("dve", [rbank[bk], r_xb], [r_xb], lambda e: e.tensor_tensor(
                                out=xb[:, c, n0:n0 + nw], in0=xb[:, c, n0:n0 + nw], in1=banks[bk][:, :nw], op=ALU.add))
                    S.dma("pool", "nxb_st", xT_v[:, :, tok0 + r_b * 64:tok0 + (r_b + RB) * 64], xb[:], [r_xb],
                          xT_regs(tok0 + r_b * 64, RB * 64))
            S.barrier()

    for layer in LAYERS:
        if DO_MIX and layer % 2 == 0:
            mla_phase(layer)
        if DO_MIX and layer % 2 == 1:
            na_phase(layer)
        if DO_PEER:
            peer_phase(layer)

    with contextlib.ExitStack() as ps:
        xtb = [sbx(ps, "xtb", [128, NCH, 512], F32) for i in range(2)]
        r_xtb = [Reg() for i in range(2)]
        sqb = sbx(ps, "sqb", [128, NCH, 512], BF16)
        r_sqb = Reg("sqb")
        rsb = sbx(ps, "rsb", [128, 512], F32)
        r_rsb = Reg("rsb")
        ynb = [sbx(ps, "ynb", [128, NCH, 512], F32) for i in range(2)]
        r_ynb = [Reg() for i in range(2)]
        yt = [sbx(ps, "yt", [128, D], F32) for i in range(2)]
        r_yt = [Reg() for i in range(2)]
        r_y = [Reg() for i in range(NTOK // 128)]
        nyt = 0
        for b in range(NB):
            s = b % 2
            S.dma("sp", f"xtbld{s}", xtb[s][:], xT_v[:, :, b * 512:(b + 1) * 512], xT_regs(b * 512, 512), [r_xtb[s]])
            rmsnorm_fm(xtb[s][:], [r_xtb[s]], NCH, 512, lambda c: gsb[:, 8, c:c + 1], ynb[s][:], r_ynb[s],
                       sqb[:], r_sqb, rsb[:], r_rsb, 4)
            for j in range(4):
                ys = nyt % 2
                nyt += 1
                for half in range(2):
                    bk = (2 * j + half) % 4
                    for cc in range(4):
                        c = half * 4 + cc
                        S.op("pe", [r_ynb[s], r_const], [rbank[bk]],
                             lambda e, c=c, j=j, cc=cc, bk=bk, s=s: e.transpose(
                                 out=banks[bk][:, cc * 128:(cc + 1) * 128], in_=ynb[s][:, c, j * 128:(j + 1) * 128],
                                 identity=identf[:]))
                    evac_copy(banks[bk][:, :], yt[ys][:, half * 512:(half + 1) * 512], [rbank[bk]], [r_yt[ys]])
                ti = b * 4 + j
                S.dma("pool", f"ytst{ys}", y_out[ti * 128:(ti + 1) * 128, :], yt[ys][:], [r_yt[ys]], [r_y[ti]])
        S.finish(r_y + [r_dbg])
    K.es = es
    K.dbg = list(dbg.keys())
    K.stats = (S.nsem, dict(S.cnt))
    return nc


def host_consts(inp, cfg):
    DEPTH = cfg["DEPTH"]
    g = np.concatenate([inp["norm_mix"], inp["norm_ffn"], inp["norm_final"][None]], axis=0)
    gains = np.ascontiguousarray(g.reshape(9, NCH, 128).transpose(2, 0, 1)).astype(np.float32)
    out = {"ident_f": np.eye(128, dtype=np.float32), "gains": gains,
           "iota_in": np.ascontiguousarray(np.broadcast_to(np.arange(128, dtype=np.float32), (128, 128)))}
    if cfg.get("DO_PEER", True):
        u = inp["peer_u"]
        L = u.shape[0]
        pu = u.reshape(L, 32, 512, NCH, 128).transpose(0, 1, 4, 3, 2).reshape(L * 32, 128, 4096)
        out["peer_u"] = np.ascontiguousarray(pu)
        v = inp["peer_v"]
        pv = v.reshape(L, 32, 4, 128, D).transpose(0, 1, 3, 2, 4).reshape(L * 32, 128, 4096)
        out["peer_v"] = np.ascontiguousarray(pv)
        wq = inp["peer_w_q"]
        pwq = wq.reshape(L, NCH, 128, 4, 512).transpose(0, 3, 2, 1, 4).reshape(L * 4, 128, 4096)
        out["peer_wq"] = np.ascontiguousarray(pwq)
        sk = inp["peer_sub_keys"]
        psk = sk.reshape(L, 16, 128, 128).transpose(0, 3, 1, 2).reshape(L, 128, 2048)
        out["peer_sk"] = np.ascontiguousarray(psk)
    if cfg.get("DO_MIX", True) and "mla_w_dq" in inp:
        Lm = inp["mla_w_dq"].shape[0]
        wdq = inp["mla_w_dq"]
        out["mla_wdq"] = np.ascontiguousarray(wdq.reshape(Lm, NCH, 128, 384).transpose(0, 2, 1, 3).reshape(Lm, 128, 3072))
        wuq = inp["mla_w_uq"]
        cols = []
        for h in range(8):
            cols += [h * 192 + d for d in range(128)]
        for h in range(8):
            cols += [h * 192 + 128 + d for d in range(64)]
        for h in range(8):
            cols += [h * 192 + 128 + (d + 32) % 64 for d in range(64)]
        wuq2 = wuq[:, :, cols]
        out["mla_wuq"] = np.ascontiguousarray(wuq2.reshape(Lm, 3, 128, 2048).transpose(0, 2, 1, 3).reshape(Lm, 128, 6144))
        wdkv = inp["mla_w_dkv"]
        cols = list(range(256)) + [256 + p % 64 for p in range(128)] + [256 + (p % 64 + 32) % 64 for p in range(128)]
        wdkv2 = wdkv[:, :, cols]
        out["mla_wdkv"] = np.ascontiguousarray(wdkv2.reshape(Lm, NCH, 128, 512).transpose(0, 2, 1, 3).reshape(Lm, 128, 4096))
        wukv = inp["mla_w_ukv"]
        out["mla_wukv"] = np.ascontiguousarray(wukv.reshape(Lm, 2, 128, 2048).transpose(0, 2, 1, 3).reshape(Lm, 128, 4096))
        wo = inp["mla_w_o"]
        out["mla_wo"] = np.ascontiguousarray(wo.reshape(Lm, 8, 128, D).transpose(0, 2, 1, 3).reshape(Lm, 128, 8192))
        g = np.concatenate([inp["mla_q_norm"].reshape(Lm, 3, 128), inp["mla_kv_norm"].reshape(Lm, 2, 128)], axis=1)
        out["mla_g"] = np.ascontiguousarray(g.transpose(0, 2, 1)).astype(np.float32)
    if cfg.get("DO_MIX", True) and "na_w_qkv" in inp:
        w = inp["na_w_qkv"]
        Ln = w.shape[0]
        out["na_wqkv"] = np.ascontiguousarray(
            w.reshape(Ln, NCH, 128, 6, 512).transpose(0, 3, 2, 1, 4).reshape(Ln * 6, 128, 4096))
        wo = inp["na_w_o"]
        out["na_wo"] = np.ascontiguousarray(wo.reshape(Ln, 8, 128, D).transpose(0, 2, 1, 3).reshape(Ln, 128, 8192))
        b = inp["na_b_qkv"]
        bq = b[:, 0:1024].reshape(Ln, 8, 128).transpose(0, 2, 1)
        bk = b[:, 1024:2048].reshape(Ln, 8, 128).transpose(0, 2, 1)
        out["na_b"] = np.ascontiguousarray(np.concatenate([bq, bk], axis=2)).astype(np.float32)
        out["na_bv"] = np.ascontiguousarray(np.broadcast_to(b[:, None, 2048:3072], (Ln, 128, D))).astype(np.float32)
        rpb = inp["na_rpb"]
        kp = np.arange(128)
        krl, kc = kp // 64, kp % 64
        cq = np.arange(64)
        c0 = np.clip(cq - 8, 0, 48)
        colok = (kc[:, None] >= c0[None, :]) & (kc[:, None] < c0[None, :] + 16)
        relc = np.clip(kc[:, None] - cq[None, :] + 15, 0, 30)

        def table(dr_of_chunk, nchunk):
            t = np.full((Ln, 128, 16, nchunk, 64), -30000.0, np.float32)
            for i in range(nchunk):
                dr = dr_of_chunk(i) + krl
                ok = (dr >= -7) & (dr <= 7)
                drc = np.clip(dr + 7, 0, 14)
                vals = rpb[:, :, drc[:, None], relc]
                m = (ok[:, None] & colok)[None, None]
                t[:, :, :, i, :] = np.where(m, vals, np.float32(-30000.0)).transpose(0, 2, 1, 3)
            t = t.reshape(Ln, 128, 8, 2, nchunk, 64).transpose(0, 1, 2, 4, 3, 5)
            return np.ascontiguousarray(t)

        out["na_BI"] = table(lambda i: -4 + 2 * i, 4).reshape(Ln, 128, 4096)
        ed = np.stack([table(lambda i, rr=rr: -4 + 2 * i - rr, 6) for rr in range(4)], axis=1)
        out["na_ED"] = np.ascontiguousarray(ed.transpose(0, 1, 3, 2, 4, 5, 6).reshape(Ln, 4, 8, 128, 768))
    return out


def na_percore(cfg, c):
    TP, TS = cfg["TP"], cfg["TS"]
    RP, RS = TP // 64, TS // 64
    qi = c % 4
    rm = np.zeros((128, 96), np.float32)
    krl = np.arange(128) // 64
    for seg in range(2):
        R = RP if seg == 0 else RS
        base = 0 if seg == 0 else qi * RS
        Rg = RP if seg == 0 else 4 * RS
        for edge in range(2):
            for rr in range(4):
                ql = rr if edge == 0 else R - 4 + rr
                rg = base + ql
                r0 = min(max(rg - 4, 0), Rg - 8)
                for i in range(6):
                    kl = (-4 + 2 * i if edge == 0 else R - 8 + 2 * i) + krl
                    kg = base + kl
                    ok = (kg >= r0) & (kg < r0 + 8)
                    rm[:, ((seg * 2 + edge) * 4 + rr) * 6 + i] = np.where(ok, 0.0, -30000.0)
    mk = np.zeros((128, 8), np.float32)
    if qi > 0:
        mk[:, qi - 1] = 1.0
    if qi < 3:
        mk[:, 4 + qi + 1] = 1.0
    return {"na_RM": rm, "na_MK": mk}


def rope_tables(positions):
    f = (np.arange(128) % 64) % 32
    inv = (np.float32(10000.0) ** (-(np.arange(0, 64, 2, dtype=np.float32)) / np.float32(64))).astype(np.float32)
    ang = (positions.astype(np.float32)[None, :] * inv[f][:, None]).astype(np.float32)
    sign = np.where((np.arange(128) % 64) < 32, -1.0, 1.0).astype(np.float32)
    return np.cos(ang).astype(np.float32), (np.sin(ang) * sign[:, None]).astype(np.float32)


def run(cfg, inp, x_prompt, x_sample):
    TP, TS = cfg["TP"], cfg["TS"]
    nc = build(cfg)
    consts = host_consts(inp, cfg)
    in_maps = []
    for c in range(NCORES):
        xs = x_sample[c // 4, (c % 4) * TS:(c % 4 + 1) * TS]
        m = dict(consts)
        m["x_in"] = np.ascontiguousarray(np.concatenate([x_prompt[c], xs], axis=0))
        if "mla_wdq" in consts:
            pos = np.concatenate([np.arange(TP), (c % 4) * TS + np.arange(TS)])
            m["rope_c"], m["rope_s"] = rope_tables(pos)
        if "na_wqkv" in consts:
            m.update(na_percore(cfg, c))
        in_maps.append(m)
    res = run_bass_kernel_spmd(nc, in_maps, core_ids=list(range(NCORES)))
    K.es.close()
    K.res = res
    yp = np.stack([res.results[c]["y_out"][:TP] for c in range(NCORES)], axis=0)
    ys = np.stack([np.concatenate([res.results[g * 4 + q]["y_out"][TP:] for q in range(4)], axis=0)
                   for g in range(2)], axis=0)
    return yp.astype(np.float32), ys.astype(np.float32)


def kernel(**inputs):
    cfg = dict(TP=2048, TS=4096, DEPTH=4)
    inp = {k: np.asarray(v) for k, v in inputs.items()}
    return run(cfg, inp, inp["x_prompt"], inp["x_sample"])
```

```python
import contextlib
import numpy as np
import ml_dtypes
import concourse.bass as bass
import concourse.mybir as mybir
from concourse.bass_utils import run_bass_kernel_spmd

F32 = mybir.dt.float32
BF16 = mybir.dt.bfloat16
U32 = mybir.dt.uint32
I32 = mybir.dt.int32
AF = mybir.ActivationFunctionType
ALU = mybir.AluOpType
AX = mybir.AxisListType

D = 1024
NCH = 8
EPS = 1e-6
NCORES = 8


class Reg:
    __slots__ = ("name", "w", "re", "rd")

    def __init__(self, name=""):
        self.name = name
        self.w = None
        self.re = {}
        self.rd = []


class Sched:
    EPOCH = 30000

    def __init__(self, nc, es):
        self.nc = nc
        self.es = es
        self.engs = {"pe": nc.tensor, "act": nc.scalar, "dve": nc.vector, "pool": nc.gpsimd, "sp": nc.sync}
        self.cnt = {e: 0 for e in self.engs}
        self.esems = {e: [] for e in self.engs}
        self.known_e = {e: {} for e in self.engs}
        self.known_d = {e: {} for e in self.engs}
        self.dsem = {}
        self.nsem = 0

    def _newsem(self, name):
        self.nsem += 1
        return self.es.enter_context(self.nc.semaphore(name))

    def _esem(self, e, ep):
        while len(self.esems[e]) <= ep:
            self.esems[e].append(self._newsem(f"e_{e}_{len(self.esems[e])}"))
        return self.esems[e][ep]

    def _wait(self, eng, tok):
        if tok[0] == "e":
            _, e2, n = tok
            if self.known_e[eng].get(e2, 0) >= n:
                return
            ep, v = divmod(n - 1, self.EPOCH)
            self.engs[eng].wait_ge(self._esem(e2, ep), v + 1)
            self.known_e[eng][e2] = n
        else:
            _, name, v = tok
            if self.known_d[eng].get(name, 0) >= v:
                return
            self.engs[eng].wait_ge(self.dsem[name][0], v)
            self.known_d[eng][name] = v

    def _deps(self, eng, reads, writes):
        toks = []
        for r in reads:
            if r.w is not None:
                toks.append(r.w)
        for w in writes:
            if w.w is not None and not (eng == "pe" and w.w[0] == "e" and w.w[1] == "pe"):
                toks.append(w.w)
            for e2, n in w.re.items():
                toks.append(("e", e2, n))
            toks.extend(w.rd)
        for t in toks:
            self._wait(eng, t)

    def op(self, eng, reads, writes, fn):
        self._deps(eng, reads, writes)
        ins = fn(self.engs[eng])
        self.cnt[eng] += 1
        n = self.cnt[eng]
        ins.then_inc(self._esem(eng, (n - 1) // self.EPOCH), 1)
        tok = ("e", eng, n)
        for r in reads:
            if r.re.get(eng, 0) < n:
                r.re[eng] = n
        for w in writes:
            w.w = tok
            w.re = {}
            w.rd = []
        return tok

    def dma(self, q, sem, out, in_, reads, writes, **kw):
        if sem not in self.dsem:
            self.dsem[sem] = [self._newsem("d_" + sem), 0]
        self._deps(q, reads, writes)
        ins = self.engs[q].dma_start(out=out, in_=in_, **kw)
        s = self.dsem[sem]
        s[1] += 16
        ins.then_inc(s[0], 16)
        tok = ("d", sem, s[1])
        for r in reads:
            r.rd.append(tok)
        for w in writes:
            w.w = tok
            w.re = {}
            w.rd = []
        return tok

    def retoken(self, regs, tok):
        for r in regs:
            r.w = tok

    def barrier(self):
        for eng in self.engs:
            for e2 in self.engs:
                if e2 != eng and self.cnt[e2] > 0:
                    self._wait(eng, ("e", e2, self.cnt[e2]))
            for name, (sem, v) in self.dsem.items():
                if v > 0:
                    self._wait(eng, ("d", name, v))

    def finish(self, regs):
        for r in regs:
            if r.w is not None:
                self._wait("sp", r.w)


class K:
    pass


def build(cfg):
    TP, TS, DEPTH = cfg["TP"], cfg["TS"], cfg["DEPTH"]
    LAYERS = cfg.get("LAYERS", list(range(DEPTH)))
    DO_MIX = cfg.get("DO_MIX", True)
    DO_PEER = cfg.get("DO_PEER", True)
    DBG = cfg.get("DBG", False)
    NTOK = TP + TS
    assert NTOK % 512 == 0
    nc = bass.Bass("TRN2", target_bir_lowering=False)
    es = contextlib.ExitStack()
    S = Sched(nc, es)
    uid = [0]

    def dram(name, shape, dt, kind="Internal"):
        return nc.dram_tensor(name, list(shape), dt, kind=kind).ap()

    def sbx(stack, name, shape, dt):
        uid[0] += 1
        return stack.enter_context(nc.sbuf_tensor(f"{name}_{uid[0]}", list(shape), dt))

    def sb(name, shape, dt):
        return sbx(es, name, shape, dt)

    x_in = dram("x_in", [NTOK, D], F32, "ExternalInput")
    y_out = dram("y_out", [NTOK, D], F32, "ExternalOutput")
    ident_f = dram("ident_f", [128, 128], F32, "ExternalInput")
    gains = dram("gains", [128, 9, NCH], F32, "ExternalInput")
    iota_in = dram("iota_in", [128, 128], F32, "ExternalInput")
    xT = dram("xT", [D, NTOK], F32)
    xT_v = xT.rearrange("(c p) t -> p c t", p=128)
    dbg = {}
    if DBG:
        for nm, shp, dt in [("d_sv", [128, 256], F32), ("d_si", [128, 256], U32), ("d_fv", [128, 128], F32),
                            ("d_fi", [128, 128], U32), ("d_ia", [128, 128], F32), ("d_ib", [128, 128], F32),
                            ("d_g", [128, 128], F32), ("d_gs", [128, 128], F32)]:
            dbg[nm] = dram(nm, shp, dt, "ExternalOutput")
    r_dbg = Reg("dbg")

    identf = sb("identf", [128, 128], F32)
    identb = sb("identb", [128, 128], BF16)
    onesb = sb("onesb", [128, 128], BF16)
    gsb = sb("gsb", [128, 9, NCH], F32)
    iotaf = sb("iotaf", [128, 128], F32)
    epsb = sb("epsb", [128, 1], F32)
    r_const = Reg("const")
    S.dma("sp", "const", identf[:], ident_f[:, :], [], [r_const])
    S.dma("sp", "const", gsb[:], gains[:, :, :], [], [r_const])
    S.dma("sp", "const", iotaf[:], iota_in[:, :], [], [r_const])
    S.retoken([r_const], ("d", "const", S.dsem["const"][1]))
    S.op("dve", [r_const], [r_const], lambda e: e.tensor_copy(out=identb[:], in_=identf[:]))
    S.op("pool", [], [r_const], lambda e: e.memset(onesb[:], 1.0))
    S.op("pool", [], [r_const], lambda e: e.memset(epsb[:], EPS))

    psall = es.enter_context(nc.psum_tensor("psall", [128, 8, 512], F32))
    banks = [psall[:, i, :] for i in range(8)]
    rbank = [Reg(f"bank{i}") for i in range(8)]

    NB = NTOK // 512
    r_xT = [Reg(f"xT{b}") for b in range(NTOK // 256)]

    def xT_regs(t0, n):
        return [r_xT[i] for i in range(t0 // 256, (t0 + n + 255) // 256)]

    evac = [0]

    def evac_copy(src, dst, rs_, rd_, eng=None):
        if eng is None:
            eng = "act" if evac[0] % 2 == 0 else "dve"
            evac[0] += 1
        if eng == "act":
            return S.op("act", rs_, rd_, lambda e: e.copy(out=dst, in_=src))
        return S.op(eng, rs_, rd_, lambda e: e.tensor_copy(out=dst, in_=src))

    cast_rr = [0]

    def precast(name, n_outer, F):
        src = dram(name, [n_outer, 128, F], F32, "ExternalInput")
        dst = dram(name + "_b", [n_outer, 128, F], BF16)
        regs = [Reg(f"{name}{i}") for i in range(n_outer)]
        return src, dst, regs

    def emit_precast(src, dst, regs, stg, r_stg, stb, r_stb, F):
        CH = 4096
        for o in range(len(regs)):
            for f0 in range(0, F, CH):
                fw = min(CH, F - f0)
                k = cast_rr[0] % 3
                cast_rr[0] += 1
                S.dma("sp", f"cst_ld{k}", stg[k][:, :fw], src[o, :, f0:f0 + fw], [], [r_stg[k]])
                eng = ["pool", "act", "dve"][k]
                evac_copy(stg[k][:, :fw], stb[k][:, :fw], [r_stg[k]], [r_stb[k]], eng=eng)
                S.dma("pool", f"cst_st{k}", dst[o, :, f0:f0 + fw], stb[k][:, :fw], [r_stb[k]], [regs[o]])

    W = {}
    if DO_PEER:
        W["pu"] = precast("peer_u", DEPTH * 32, 4096)
        W["pv"] = precast("peer_v", DEPTH * 32, 4096)
        W["pwq"] = precast("peer_wq", DEPTH * 4, 4096)
        W["psk"] = precast("peer_sk", DEPTH, 2048)
    SCALE = (128 + 64) ** -0.5
    if DO_MIX and any(l % 2 == 0 for l in LAYERS):
        Lm = (DEPTH + 1) // 2
        W["wdq"] = precast("mla_wdq", Lm, 3072)
        W["wuq"] = precast("mla_wuq", Lm, 6144)
        W["wdkv"] = precast("mla_wdkv", Lm, 4096)
        W["wukv"] = precast("mla_wukv", Lm, 4096)
        W["wo"] = precast("mla_wo", Lm, 8192)
        mla_g = dram("mla_g", [Lm, 128, 5], F32, "ExternalInput")
        rope_c = dram("rope_c", [128, NTOK], F32, "ExternalInput")
        rope_s = dram("rope_s", [128, NTOK], F32, "ExternalInput")
        qn_d = dram("qn_d", [8, 128, NTOK], BF16)
        qp_d = dram("qp_d", [4, 128, NTOK], BF16)
        oT_d = dram("oT_d", [8, 128, NTOK], BF16)
        latP = dram("latP", [384, TP], BF16)
        CW = min(TS, 1024)
        NCHK = TS // CW
        latS_k = [dram(f"latS{k}", [384, CW], BF16) for k in range(NCHK)]
        latA_k = [dram(f"latA{k}", [4 * 384, CW], BF16) for k in range(NCHK)]
        r_qn = [Reg() for _ in range(NTOK // 512)]
        r_oT = [Reg() for _ in range(NTOK // 512)]
        r_latP = Reg()
        r_latS = Reg()
        r_latA = Reg()


    if DO_MIX and any(l % 2 == 1 for l in LAYERS):
        Ln = DEPTH // 2
        W["nqkv"] = precast("na_wqkv", Ln * 6, 4096)
        W["nwo"] = precast("na_wo", Ln, 8192)
        na_b = dram("na_b", [Ln, 128, 16], F32, "ExternalInput")
        na_bv = dram("na_bv", [Ln, 128, D], F32, "ExternalInput")
        na_BI = dram("na_BI", [Ln, 128, 4096], F32, "ExternalInput")
        na_ED = dram("na_ED", [Ln, 4, 8, 128, 768], F32, "ExternalInput")
        na_RM = dram("na_RM", [128, 96], F32, "ExternalInput")
        na_MK = dram("na_MK", [128, 8], F32, "ExternalInput")
        RP, RS_ = TP // 64, TS // 64
        EXTP, EXTS = (RP + 8) * 64, (RS_ + 8) * 64
        naq = dram("naq", [8, 128, NTOK], BF16)
        nakE = [dram("nakEp", [8, 128, EXTP], BF16), dram("nakEs", [8, 128, EXTS], BF16)]
        navE = [dram("navEp", [EXTP, D], BF16), dram("navEs", [EXTS, D], BF16)]
        sendK_h = [dram(f"sendK{g}", [512, 512], BF16) for g in range(2)]
        sendV_h = [dram(f"sendV{g}", [256, D], BF16) for g in range(2)]
        recvK_h = [dram(f"recvK{g}", [4 * 512, 512], BF16) for g in range(2)]
        recvV_h = [dram(f"recvV{g}", [4 * 256, D], BF16) for g in range(2)]
        r_naq = Reg(); r_nak = [Reg(), Reg()]; r_nav = [Reg(), Reg()]
        r_send = Reg(); r_recv = Reg()
    with contextlib.ExitStack() as ps:
        stg = [sbx(ps, "stg", [128, 4096], F32) for _ in range(3)]
        stb = [sbx(ps, "stb", [128, 4096], BF16) for _ in range(3)]
        r_stg = [Reg() for _ in range(3)]
        r_stb = [Reg() for _ in range(3)]
        for key, (src, dst, regs) in W.items():
            Fk = src.shape[2]
            if key in ("pu", "pv", "pwq", "psk"):
                sel = [o for o in range(len(regs))
                       if (o // (len(regs) // DEPTH)) in LAYERS]
                for o in sel:
                    emit_precast(src[o:o + 1], dst[o:o + 1], regs[o:o + 1], stg, r_stg, stb, r_stb, Fk)
            else:
                emit_precast(src, dst, regs, stg, r_stg, stb, r_stb, Fk)
        S.barrier()

    with contextlib.ExitStack() as ps:
        xin = [sbx(ps, "xin", [128, 4, D], F32) for i in range(2)]
        r_xin = [Reg() for i in range(2)]
        xtb = [sbx(ps, "xtb", [128, NCH, 512], F32) for i in range(2)]
        r_xtb = [Reg() for i in range(2)]
        x_in_v = x_in.rearrange("(b j p) d -> b p j d", j=4, p=128)
        for b in range(NB):
            s = b % 2
            S.dma("sp", f"xin{s}", xin[s][:], x_in_v[b], [], [r_xin[s]])
            for j in range(4):
                for half in range(2):
                    bk = (2 * j + half) % 4
                    for cc in range(4):
                        c = half * 4 + cc
                        S.op("pe", [r_xin[s], r_const], [rbank[bk]],
                             lambda e, c=c, j=j, cc=cc, bk=bk, s=s: e.transpose(
                                 out=banks[bk][:, cc * 128:(cc + 1) * 128],
                                 in_=xin[s][:, j, c * 128:(c + 1) * 128], identity=identf[:]))
                    evac_copy(banks[bk][:, :].rearrange("p (c t) -> p c t", c=4),
                              xtb[s][:, half * 4:(half + 1) * 4, j * 128:(j + 1) * 128], [rbank[bk]], [r_xtb[s]])
            S.dma("pool", f"xtbst{s}", xT_v[:, :, b * 512:(b + 1) * 512], xtb[s][:], [r_xtb[s]],
                  xT_regs(b * 512, 512))
        S.barrier()

    def rmsnorm_fm(src, r_src, nchunk, T, gcol, dst, r_dst, sq, r_sq, rs, r_rs, bk, nfeat=None):
        nfeat = nfeat or nchunk * 128
        S.op("act", r_src, [r_sq], lambda e: e.activation(out=sq, in_=src, func=AF.Square))
        for c in range(nchunk):
            S.op("pe", [r_sq, r_const], [rbank[bk]],
                 lambda e, c=c: e.matmul(banks[bk][:, :T], lhsT=onesb[:], rhs=sq[:, c, :],
                                         start=(c == 0), stop=(c == nchunk - 1)))
        S.op("act", [rbank[bk], r_const], [r_rs],
             lambda e: e.activation(out=rs, in_=banks[bk][:, :T], func=AF.Sqrt, scale=1.0 / nfeat,
                                    bias=epsb[:, 0:1]))
        S.op("dve", [r_rs], [r_rs], lambda e: e.reciprocal(out=rs, in_=rs))
        for c in range(nchunk):
            S.op("dve", r_src + [r_rs, r_const], [r_dst],
                 lambda e, c=c: e.scalar_tensor_tensor(out=dst[:, c, :], in0=src[:, c, :], scalar=gcol(c),
                                                       in1=rs, op0=ALU.mult, op1=ALU.mult))

    def peer_phase(layer):
        TB = 256
        nblk = NTOK // TB
        if cfg.get("PEER_BLOCKS") is not None:
            nblk = cfg["PEER_BLOCKS"]
        NGRP = cfg.get("PEER_NGRP", 32)
        _, pu_b, r_pu = W["pu"]
        _, pv_b, r_pv = W["pv"]
        _, pwq_b, r_pwq = W["pwq"]
        _, psk_b, r_psk = W["psk"]
        with contextlib.ExitStack() as ps:
            A = lambda name, shape, dt: sbx(ps, name, shape, dt)
            wq = A("wq", [128, NCH, 2048], BF16)
            r_wq = Reg()
            skT = A("skT", [128, 16, 128], BF16)
            r_sk = Reg()
            for gq in range(4):
                S.dma("sp", "pw_ld", wq[:, :, gq * 512:(gq + 1) * 512],
                      pwq_b[layer * 4 + gq].rearrange("p (c n) -> p c n", c=NCH), [r_pwq[layer * 4 + gq]], [r_wq])
            S.dma("sp", "pw_ld", skT[:], psk_b[layer].rearrange("p (g k) -> p g k", g=16), [r_psk[layer]], [r_sk])
            tokf = ("d", "pw_ld", S.dsem["pw_ld"][1])
            S.retoken([r_wq, r_sk], tokf)
            xb = [A("xb", [128, NCH, TB], F32) for _ in range(2)]; r_xb = [Reg() for _ in range(2)]
            hb = [A("hb", [128, NCH, TB], BF16) for _ in range(2)]; r_hb = [Reg() for _ in range(2)]
            sq = A("sq", [128, NCH, TB], BF16); r_sq = Reg()
            rs = A("rs", [128, TB], F32); r_rs = Reg()
            qT = A("qT", [128, 16, TB], BF16); r_qT = Reg()
            sc = A("sc", [128, 2048], F32); r_sc = Reg()
            cand = sc[:].rearrange("q (h n) -> q h n", h=8); r_cand = r_sc
            eq = sc[:].rearrange("q (s a) -> q s a", a=16); r_eq = r_sc
            work = A("work", [128, 256], F32); r_work = Reg()
            sv = A("sv", [128, 16, 16], F32); r_sv = Reg()
            si = A("si", [128, 16, 16], U32); r_si = Reg()
            sif = A("sif", [128, 16, 16], F32); r_sif = Reg()
            fv = A("fv", [128, 8, 16], F32); r_fv = Reg()
            fi = A("fi", [128, 8, 16], U32); r_fi = Reg()
            ab_u = A("ab_u", [128, 2, 128], U32); r_abu = Reg()
            ab_f = A("ab_f", [128, 2, 128], F32); r_abf = Reg()
            iab = A("iab", [128, 3, 128], F32); r_iab = Reg()
            gtmp = A("gtmp", [128, 8, 16], F32); r_gtmp = Reg()
            gsum = A("gsum", [128, 8], F32); r_gsum = Reg()
            slT = [A("slT", [128, 3, TB], F32) for _ in range(2)]; r_slT = [Reg() for _ in range(2)]
            NE, EG = 2, 4
            E1 = [A("E1", [128, EG, 128], BF16) for _ in range(NE)]; r_E1 = [Reg() for _ in range(NE)]
            E0 = [A("E0", [128, EG, 128], BF16) for _ in range(NE)]; r_E0 = [Reg() for _ in range(NE)]
            Gs = A("Gs", [128, TB, 128], BF16); r_Gs = Reg()
            NU = 2
            Ut = [A("Ut", [128, NCH, 512], BF16) for _ in range(NU)]; r_Ut = [Reg() for _ in range(NU)]
            Vt = [A("Vt", [128, 4, D], BF16) for _ in range(NU)]; r_Vt = [Reg() for _ in range(NU)]
            gel = [A("gel", [128, TB], F32) for _ in range(2)]; r_gel = [Reg() for _ in range(2)]
            wT = [A("wT", [128, TB], BF16) for _ in range(3)]; r_wT = [Reg() for _ in range(3)]
            osb = A("osb", [128, D], F32); r_osb = Reg()
            iota16 = iotaf[:, 0:16]
            st = {"ec": 0}

            def prep1(blk):
                p = blk % 2
                t0 = blk * TB
                S.dma("sp", f"pxb_ld{p}", xb[p][:], xT_v[:, :, t0:t0 + TB], xT_regs(t0, TB), [r_xb[p]])
                rmsnorm_fm(xb[p][:], [r_xb[p]], NCH, TB, lambda c: gsb[:, 4 + layer, c:c + 1], hb[p][:], r_hb[p],
                           sq[:], r_sq, rs[:], r_rs, 2)
                for g in range(16):
                    bk = 2 + (g // 2) % 2
                    off = (g % 2) * TB
                    for c in range(NCH):
                        S.op("pe", [r_wq, r_hb[p]], [rbank[bk]], lambda e: e.matmul(
                            banks[bk][:, off:off + TB], lhsT=wq[:, c, g * 128:(g + 1) * 128], rhs=hb[p][:, c, :],
                            start=(c == 0), stop=(c == NCH - 1)))
                    if g % 2 == 1:
                        evac_copy(banks[bk][:, :].rearrange("p (g t) -> p g t", g=2), qT[:, g - 1:g + 1, :],
                                  [rbank[bk]], [r_qT], eng="act")
                    yield
                for tt in range(TB // 128):
                    ts_ = slice(tt * 128, (tt + 1) * 128)
                    for g in range(16):
                        bk = 2 + (g // 4) % 2
                        S.op("pe", [r_qT, r_sk], [rbank[bk]], lambda e: e.matmul(
                            banks[bk][:, (g % 4) * 128:(g % 4 + 1) * 128], lhsT=qT[:, g, ts_], rhs=skT[:, g, :],
                            start=True, stop=True))
                        if g % 4 == 3:
                            evac_copy(banks[bk][:, :], sc[:, (g // 4) * 512:(g // 4 + 1) * 512], [rbank[bk]], [r_sc],
                                      eng="act")
                    for g in range(16):
                        sg = sc[:, g * 128:(g + 1) * 128]
                        S.op("dve", [r_sc], [r_sv], lambda e: e.max(out=sv[:, g, 0:8], in_=sg))
                        S.op("dve", [r_sc, r_sv], [r_si],
                             lambda e: e.max_index(out=si[:, g, 0:8], in_max=sv[:, g, 0:8], in_values=sg))
                        S.op("dve", [r_sc, r_sv], [r_work],
                             lambda e: e.match_replace(out=work[:, 0:128], in_to_replace=sv[:, g, 0:8],
                                                       in_values=sg, imm_value=-1e30))
                        S.op("dve", [r_work], [r_sv], lambda e: e.max(out=sv[:, g, 8:16], in_=work[:, 0:128]))
                        S.op("dve", [r_work, r_sv], [r_si],
                             lambda e: e.max_index(out=si[:, g, 8:16], in_max=sv[:, g, 8:16],
                                                   in_values=work[:, 0:128]))
                        yield
                    svv = sv[:].rearrange("q (h p) k -> q h p k", p=2)
                    c4 = cand.rearrange("q h (a b) -> q h a b", a=16)
                    S.op("dve", [r_sv], [r_cand], lambda e: e.tensor_tensor(
                        out=c4, in0=svv[:, :, 0, :].unsqueeze(3).to_broadcast([128, 8, 16, 16]),
                        in1=svv[:, :, 1, :].unsqueeze(2).to_broadcast([128, 8, 16, 16]), op=ALU.add))
                    for h in range(8):
                        ch = cand[:, h, :]
                        S.op("dve", [r_cand], [r_fv], lambda e: e.max(out=fv[:, h, 0:8], in_=ch))
                        S.op("dve", [r_cand, r_fv], [r_fi],
                             lambda e: e.max_index(out=fi[:, h, 0:8], in_max=fv[:, h, 0:8], in_values=ch))
                        S.op("dve", [r_cand, r_fv], [r_work],
                             lambda e: e.match_replace(out=work[:], in_to_replace=fv[:, h, 0:8],
                                                       in_values=ch, imm_value=-1e30))
                        S.op("dve", [r_work], [r_fv], lambda e: e.max(out=fv[:, h, 8:16], in_=work[:]))
                        S.op("dve", [r_work, r_fv], [r_fi],
                             lambda e: e.max_index(out=fi[:, h, 8:16], in_max=fv[:, h, 8:16], in_values=work[:]))
                        yield
                    fi2 = fi[:].rearrange("q h k -> q (h k)")
                    S.op("dve", [r_fi], [r_abu], lambda e: e.tensor_single_scalar(
                        out=ab_u[:, 0, :], in_=fi2, scalar=4, op=ALU.logical_shift_right))
                    S.op("dve", [r_fi], [r_abu], lambda e: e.tensor_single_scalar(
                        out=ab_u[:, 1, :], in_=fi2, scalar=15, op=ALU.bitwise_and))
                    S.op("dve", [r_abu], [r_abf], lambda e: e.tensor_copy(out=ab_f[:], in_=ab_u[:]))
                    S.op("dve", [r_si], [r_sif], lambda e: e.tensor_copy(out=sif[:], in_=si[:]))
                    sifv = sif[:].rearrange("q (h p) k -> q h p k", p=2)
                    eq4 = eq.rearrange("q (h k) a -> q h k a", h=8)
                    for p_ in range(2):
                        abv = ab_f[:, p_, :].rearrange("q (h k) -> q h k", h=8)
                        S.op("dve", [r_abf, r_const], [r_eq], lambda e: e.tensor_tensor(
                            out=eq4, in0=abv.unsqueeze(3).to_broadcast([128, 8, 16, 16]),
                            in1=iota16.unsqueeze(1).unsqueeze(1).to_broadcast([128, 8, 16, 16]), op=ALU.is_equal))
                        S.op("dve", [r_eq, r_sif], [r_eq], lambda e: e.tensor_tensor(
                            out=eq4, in0=eq4, in1=sifv[:, :, p_, :].unsqueeze(2).to_broadcast([128, 8, 16, 16]),
                            op=ALU.mult))
                        S.op("dve", [r_eq], [r_iab], lambda e: e.tensor_reduce(
                            out=iab[:, p_, :], in_=eq, axis=AX.X, op=ALU.add))
                        yield
                    S.op("dve", [r_fv], [r_gtmp], lambda e: e.tensor_tensor(
                        out=gtmp[:], in0=fv[:], in1=fv[:, :, 0:1].to_broadcast([128, 8, 16]), op=ALU.subtract))
                    S.op("act", [r_gtmp], [r_gtmp], lambda e: e.activation(out=gtmp[:], in_=gtmp[:], func=AF.Exp))
                    S.op("dve", [r_gtmp], [r_gsum], lambda e: e.tensor_reduce(
                        out=gsum[:], in_=gtmp[:], axis=AX.X, op=ALU.add))
                    S.op("dve", [r_gsum], [r_gsum], lambda e: e.reciprocal(out=gsum[:], in_=gsum[:]))
                    S.op("dve", [r_gtmp, r_gsum], [r_iab], lambda e: e.tensor_tensor(
                        out=iab[:, 2, :].rearrange("q (h k) -> q h k", h=8), in0=gtmp[:],
                        in1=gsum[:].unsqueeze(2).to_broadcast([128, 8, 16]), op=ALU.mult))
                    for q3 in range(3):
                        S.op("pe", [r_iab, r_const], [rbank[2]], lambda e: e.transpose(
                            out=banks[2][:, q3 * 128:(q3 + 1) * 128], in_=iab[:, q3, :], identity=identf[:]))
                    evac_copy(banks[2][:, 0:384].rearrange("p (q t) -> p q t", q=3), slT[p][:, :, ts_], [rbank[2]],
                              [r_slT[p]], eng="act")

            def prep2(blk):
                p = blk % 2
                for t4 in range(TB // EG):
                    t = t4 * EG
                    k = st["ec"] % NE
                    st["ec"] += 1
                    iob = iotaf[:].unsqueeze(1).to_broadcast([128, EG, 128])
                    S.op("dve", [r_slT[p], r_const], [r_E1[k]], lambda e: e.tensor_tensor(
                        out=E1[k][:], in0=iob, in1=slT[p][:, 1, t:t + EG].unsqueeze(2).to_broadcast([128, EG, 128]),
                        op=ALU.is_equal))
                    S.op("dve", [r_slT[p], r_const], [r_E0[k]], lambda e: e.tensor_tensor(
                        out=E0[k][:], in0=iob, in1=slT[p][:, 0, t:t + EG].unsqueeze(2).to_broadcast([128, EG, 128]),
                        op=ALU.is_equal))
                    S.op("dve", [r_slT[p], r_E0[k]], [r_E0[k]], lambda e: e.tensor_tensor(
                        out=E0[k][:], in0=E0[k][:], in1=slT[p][:, 2, t:t + EG].unsqueeze(2).to_broadcast([128, EG, 128]),
                        op=ALU.mult))
                    bk = 2 + t4 % 2
                    for tl in range(EG):
                        S.op("pe", [r_E1[k], r_E0[k]], [rbank[bk]], lambda e: e.matmul(
                            banks[bk][:, tl * 128:(tl + 1) * 128], lhsT=E1[k][:, tl, :], rhs=E0[k][:, tl, :],
                            start=True, stop=True))
                    evac_copy(banks[bk][:, :].rearrange("p (t i) -> p t i", t=4), Gs[:, t:t + EG, :],
                              [rbank[bk]], [r_Gs], eng="act")

            def loop(blk):
                p = blk % 2
                nexp = NGRP * 4
                gen = prep1(blk + 1) if blk + 1 < nblk else None
                for grp in range(NGRP):
                    us = grp % NU
                    o = layer * 32 + grp
                    S.dma("sp", f"pu_ld{us}", Ut[us][:], pu_b[o].rearrange("p (c e) -> p c e", c=NCH),
                          [r_pu[o]], [r_Ut[us]])
                    S.dma("sp", f"pv_ld{us}", Vt[us][:], pv_b[o].rearrange("p (i d) -> p i d", i=4),
                          [r_pv[o]], [r_Vt[us]])
                    for il in range(4):
                        i = grp * 4 + il
                        ba = i % 2
                        for c in range(NCH):
                            S.op("pe", [r_Ut[us], r_hb[p]], [rbank[ba]], lambda e: e.matmul(
                                banks[ba][:, :TB], lhsT=Ut[us][:, c, il * 128:(il + 1) * 128], rhs=hb[p][:, c, :],
                                start=(c == 0), stop=(c == NCH - 1)))
                        gk = i % 2
                        S.op("act", [rbank[ba]], [r_gel[gk]], lambda e: e.activation(
                            out=gel[gk][:], in_=banks[ba][:, :TB], func=AF.Gelu))
                        wk = i % 3
                        S.op("dve", [r_gel[gk], r_Gs], [r_wT[wk]], lambda e: e.tensor_tensor(
                            out=wT[wk][:], in0=gel[gk][:], in1=Gs[:, :, i], op=ALU.mult))
                        if gen is not None:
                            next(gen, None)
                        for tt in range(TB // 128):
                            for half in range(2):
                                bk = 4 + tt * 2 + half
                                S.op("pe", [r_wT[wk], r_Vt[us]], [rbank[bk]], lambda e: e.matmul(
                                    banks[bk][:, :], lhsT=wT[wk][:, tt * 128:(tt + 1) * 128],
                                    rhs=Vt[us][:, il, half * 512:(half + 1) * 512],
                                    start=(i == 0), stop=(i == nexp - 1)))
                if gen is not None:
                    for _ in gen:
                        pass

            def fin(blk):
                p = blk % 2
                t0 = blk * TB
                for tt in range(TB // 128):
                    for half in range(2):
                        bk = 4 + tt * 2 + half
                        evac_copy(banks[bk][:, :], osb[:, half * 512:(half + 1) * 512], [rbank[bk]], [r_osb])
                    for half in range(2):
                        bk = 2 + half
                        for cc in range(4):
                            c = half * 4 + cc
                            S.op("pe", [r_osb, r_const], [rbank[bk]], lambda e: e.transpose(
                                out=banks[bk][:, cc * 128:(cc + 1) * 128], in_=osb[:, c * 128:(c + 1) * 128],
                                identity=identf[:]))
                        S.op("dve", [rbank[bk], r_xb[p]], [r_xb[p]], lambda e: e.tensor_tensor(
                            out=xb[p][:, half * 4:(half + 1) * 4, tt * 128:(tt + 1) * 128],
                            in0=xb[p][:, half * 4:(half + 1) * 4, tt * 128:(tt + 1) * 128],
                            in1=banks[bk][:, :].rearrange("p (c t) -> p c t", c=4), op=ALU.add))
                S.dma("pool", f"pxb_st{p}", xT_v[:, :, t0:t0 + TB], xb[p][:], [r_xb[p]], xT_regs(t0, TB))

            for _ in prep1(0):
                pass
            for blk in range(nblk):
                prep2(blk)
                loop(blk)
                fin(blk)
            S.barrier()

    def mla_phase(layer):
        j = layer // 2
        _, wdq_b, r_wdq = W["wdq"]
        _, wuq_b, r_wuq = W["wuq"]
        _, wdkv_b, r_wdkv = W["wdkv"]
        _, wukv_b, r_wukv = W["wukv"]
        _, wo_b, r_wo = W["wo"]
        TBK = 512
        with contextlib.ExitStack() as ps:
            A = lambda name, shape, dt: sbx(ps, name, shape, dt)
            wdq = A("wdq", [128, NCH, 384], BF16)
            wuq = A("wuq", [128, 3, 2048], BF16)
            wdkv = A("wdkv", [128, NCH, 512], BF16)
            mg = A("mg", [128, 5], F32)
            r_w = Reg()
            S.dma("sp", "mw_ld", wdq[:], wdq_b[j].rearrange("p (c n) -> p c n", c=NCH), [r_wdq[j]], [r_w])
            S.dma("sp", "mw_ld", wuq[:], wuq_b[j].rearrange("p (c n) -> p c n", c=3), [r_wuq[j]], [r_w])
            S.dma("sp", "mw_ld", wdkv[:], wdkv_b[j].rearrange("p (c n) -> p c n", c=NCH), [r_wdkv[j]], [r_w])
            S.dma("sp", "mw_ld", mg[:], mla_g[j], [], [r_w])
            S.retoken([r_w], ("d", "mw_ld", S.dsem["mw_ld"][1]))
            xb = A("xb", [128, NCH, TBK], F32); r_xb = Reg()
            hb = A("hb", [128, NCH, TBK], BF16); r_hb = Reg()
            sq = A("sq", [128, NCH, TBK], BF16); r_sq = Reg()
            rs = A("rs", [128, TBK], F32); r_rs = Reg()
            cqn = A("cqn", [128, 3, TBK], BF16); r_cqn = Reg()
            cs = A("cs", [128, 2, TBK], F32); r_cs = Reg()
            qst = [A("qst", [128, 8, TBK], BF16) for _ in range(2)]; r_qst = [Reg() for _ in range(2)]
            pst = [A("pst", [128, 4, TBK], BF16) for _ in range(2)]; r_pst = [Reg() for _ in range(2)]
            lst = [A("lst", [128, 3, TBK], BF16) for _ in range(2)]; r_lst = [Reg() for _ in range(2)]
            t1 = A("t1", [128, TBK], F32); r_t1 = Reg()
            t2 = A("t2", [128, TBK], F32); r_t2 = Reg()
            for blk in range(NTOK // TBK):
                t0 = blk * TBK
                s2 = blk % 2
                S.dma("sp", "mxb_ld", xb[:], xT_v[:, :, t0:t0 + TBK], xT_regs(t0, TBK), [r_xb])
                S.dma("sp", "mcs_ld", cs[:, 0, :], rope_c[:, t0:t0 + TBK], [], [r_cs])
                S.dma("sp", "mcs_ld", cs[:, 1, :], rope_s[:, t0:t0 + TBK], [], [r_cs])
                S.retoken([r_cs], ("d", "mcs_ld", S.dsem["mcs_ld"][1]))
                rmsnorm_fm(xb[:], [r_xb], NCH, TBK, lambda c: gsb[:, layer, c:c + 1], hb[:], r_hb,
                           sq[:], r_sq, rs[:], r_rs, 0)
                for cc in range(3):
                    for c in range(NCH):
                        S.op("pe", [r_w, r_hb], [rbank[1 + cc]], lambda e, cc=cc, c=c: e.matmul(
                            banks[1 + cc][:, :], lhsT=wdq[:, c, cc * 128:(cc + 1) * 128], rhs=hb[:, c, :],
                            start=(c == 0), stop=(c == NCH - 1)))
                rmsnorm_fm(psall[:, 1:4, :], [rbank[1], rbank[2], rbank[3]], 3, TBK, lambda c: mg[:, c:c + 1],
                           cqn[:], r_cqn, sq[:, 0:3, :], r_sq, rs[:], r_rs, 0)
                for h in range(8):
                    bk = 4 + h % 2
                    for c in range(3):
                        S.op("pe", [r_w, r_cqn], [rbank[bk]], lambda e, h=h, c=c, bk=bk: e.matmul(
                            banks[bk][:, :], lhsT=wuq[:, c, h * 128:(h + 1) * 128], rhs=cqn[:, c, :],
                            start=(c == 0), stop=(c == 2)))
                    evac_copy(banks[bk][:, :], qst[s2][:, h, :], [rbank[bk]], [r_qst[s2]])
                S.dma("pool", f"mq_st{s2}", qn_d[:, :, t0:t0 + TBK].rearrange("h p t -> p h t"), qst[s2][:],
                      [r_qst[s2]], [r_qn[blk]])
                for jp in range(4):
                    for c in range(3):
                        S.op("pe", [r_w, r_cqn], [rbank[6]], lambda e, jp=jp, c=c: e.matmul(
                            banks[6][:, :], lhsT=wuq[:, c, 1024 + jp * 128:1024 + (jp + 1) * 128], rhs=cqn[:, c, :],
                            start=(c == 0), stop=(c == 2)))
                    for c in range(3):
                        S.op("pe", [r_w, r_cqn], [rbank[7]], lambda e, jp=jp, c=c: e.matmul(
                            banks[7][:, :], lhsT=wuq[:, c, 1536 + jp * 128:1536 + (jp + 1) * 128], rhs=cqn[:, c, :],
                            start=(c == 0), stop=(c == 2)))
                    S.op("dve", [rbank[6], r_cs], [r_t1], lambda e: e.tensor_tensor(
                        out=t1[:], in0=banks[6][:, :], in1=cs[:, 0, :], op=ALU.mult))
                    S.op("dve", [rbank[7], r_cs], [r_t2], lambda e: e.tensor_tensor(
                        out=t2[:], in0=banks[7][:, :], in1=cs[:, 1, :], op=ALU.mult))
                    S.op("pool", [r_t1, r_t2], [r_pst[s2]], lambda e, jp=jp: e.tensor_tensor(
                        out=pst[s2][:, jp, :], in0=t1[:], in1=t2[:], op=ALU.add))
                S.dma("pool", f"mp_st{s2}", qp_d[:, :, t0:t0 + TBK].rearrange("h p t -> p h t"), pst[s2][:],
                      [r_pst[s2]], [r_qn[blk]])
                for cc in range(4):
                    for c in range(NCH):
                        S.op("pe", [r_w, r_hb], [rbank[1 + cc]], lambda e, cc=cc, c=c: e.matmul(
                            banks[1 + cc][:, :], lhsT=wdkv[:, c, cc * 128:(cc + 1) * 128], rhs=hb[:, c, :],
                            start=(c == 0), stop=(c == NCH - 1)))
                rmsnorm_fm(psall[:, 1:3, :], [rbank[1], rbank[2]], 2, TBK, lambda c: mg[:, 3 + c:4 + c],
                           lst[s2][:, 0:2, :], r_lst[s2], sq[:, 0:2, :], r_sq, rs[:], r_rs, 0)
                S.op("dve", [rbank[3], r_cs], [r_t1], lambda e: e.tensor_tensor(
                    out=t1[:], in0=banks[3][:, :], in1=cs[:, 0, :], op=ALU.mult))
                S.op("dve", [rbank[4], r_cs], [r_t2], lambda e: e.tensor_tensor(
                    out=t2[:], in0=banks[4][:, :], in1=cs[:, 1, :], op=ALU.mult))
                S.op("pool", [r_t1, r_t2], [r_lst[s2]], lambda e: e.tensor_tensor(
                    out=lst[s2][:, 2, :], in0=t1[:], in1=t2[:], op=ALU.add))
                if t0 < TP:
                    dst = latP[:, t0:t0 + TBK].rearrange("(c p) t -> p c t", p=128)
                    S.dma("pool", f"ml_st{s2}", dst, lst[s2][:], [r_lst[s2]], [r_latP])
                else:
                    lt0 = t0 - TP
                    dst = latS_k[lt0 // CW][:, lt0 % CW:lt0 % CW + TBK].rearrange("(c p) t -> p c t", p=128)
                    S.dma("pool", f"ml_st{s2}", dst, lst[s2][:], [r_lst[s2]], [r_latS])
            S.barrier()
        if cfg.get("USE_CC", True):
            for k in range(NCHK):
                cc_allgather(latS_k[k][:, :], latA_k[k][:, :], r_latS, r_latA)
            S.barrier()
        with contextlib.ExitStack() as ps:
            A = lambda name, shape, dt: sbx(ps, name, shape, dt)
            TKMAX = max(TP, 4 * TS)
            wukv = A("wukv", [128, 2, 2048], BF16); r_wk = Reg()
            S.dma("sp", "mw_ld", wukv[:], wukv_b[j].rearrange("p (c n) -> p c n", c=2), [r_wukv[j]], [r_wk])
            kpe2 = A("kpe2", [128, TKMAX], BF16); r_kpe = Reg()
            KT = A("KT", [128, TKMAX], BF16); r_KT = Reg()
            Vh = A("Vh", [128, TKMAX // 128, 128], BF16); r_Vh = Reg()
            ckb = [A("ckb", [128, 2, 512], BF16) for _ in range(2)]; r_ckb = [Reg() for _ in range(2)]
            Qn = [A("Qn", [128, 512], BF16) for _ in range(2)]; r_Qn = [Reg() for _ in range(2)]
            Qp = [A("Qp", [128, 512], BF16) for _ in range(2)]; r_Qp = [Reg() for _ in range(2)]
            pT = [A("pT", [128, 512], BF16) for _ in range(3)]; r_pT = [Reg() for _ in range(3)]
            rden = A("rden", [128, 512], F32); r_rden = Reg()
            ost = [A("ost", [128, 512], BF16) for _ in range(2)]; r_ost = [Reg() for _ in range(2)]
            nq = 0
            ncb = 0
            for seg in range(2):
                if seg == 0:
                    Tq, Tk, q0 = TP, TP, 0
                    lat_rows = lambda r0, r1, k0, k1: latP[r0:r1, k0:k1]
                    r_lat = r_latP
                else:
                    Tq, Tk, q0 = TS, 4 * TS, TP
                    def lat_rows(r0, r1, k0, k1):
                        rk = k0 // TS
                        w0 = k0 - rk * TS
                        kk, off = w0 // CW, w0 % CW
                        assert off + (k1 - k0) <= CW
                        return latA_k[kk][rk * 384 + r0:rk * 384 + r1, off:off + (k1 - k0)]
                    r_lat = r_latA
                w_ = CW if seg else TP
                for rk in range(Tk // w_):
                    S.dma("sp", "mk_ld", kpe2[:, rk * w_:(rk + 1) * w_], lat_rows(256, 384, rk * w_, (rk + 1) * w_),
                          [r_lat], [r_kpe])
                S.retoken([r_kpe], ("d", "mk_ld", S.dsem["mk_ld"][1]))
                for h in range(8):
                    for kb in range(Tk // 512):
                        cs_ = ncb % 2
                        ncb += 1
                        S.dma("sp", f"mc_ld{cs_}", ckb[cs_][:],
                              lat_rows(0, 256, kb * 512, (kb + 1) * 512).rearrange("(c p) t -> p c t", p=128),
                              [r_lat], [r_ckb[cs_]])
                        for c in range(2):
                            S.op("pe", [r_wk, r_ckb[cs_]], [rbank[0]], lambda e, c=c, cs_=cs_, h=h: e.matmul(
                                banks[0][:, :], lhsT=wukv[:, c, h * 256:h * 256 + 128], rhs=ckb[cs_][:, c, :],
                                start=(c == 0), stop=(c == 1)))
                        evac_copy(banks[0][:, :], KT[:, kb * 512:(kb + 1) * 512], [rbank[0]], [r_KT])
                        for kt in range(4):
                            for c in range(2):
                                S.op("pe", [r_wk, r_ckb[cs_]], [rbank[1]], lambda e, c=c, cs_=cs_, h=h, kt=kt: e.matmul(
                                    banks[1][:, kt * 128:(kt + 1) * 128], lhsT=ckb[cs_][:, c, kt * 128:(kt + 1) * 128],
                                    rhs=wukv[:, c, h * 256 + 128:h * 256 + 256], start=(c == 0), stop=(c == 1)))
                        evac_copy(banks[1][:, :].rearrange("p (k d) -> p k d", k=4), Vh[:, kb * 4:(kb + 1) * 4, :],
                                  [rbank[1]], [r_Vh])
                    hp = (h % 2) * 64
                    for qb in range(Tq // 512):
                        qs = nq % 2
                        nq += 1
                        tq0 = q0 + qb * 512
                        S.dma("sp", f"mqn_ld{qs}", Qn[qs][:], qn_d[h, :, tq0:tq0 + 512], [r_qn[tq0 // 512]], [r_Qn[qs]])
                        S.dma("sp", f"mqp_ld{qs}", Qp[qs][:], qp_d[h // 2, :, tq0:tq0 + 512], [r_qn[tq0 // 512]],
                              [r_Qp[qs]])
                        nkt = Tk // 128

                        def s_mm(kt):
                            bs = 2 + kt % 3
                            S.op("pe", [r_KT, r_Qn[qs]], [rbank[bs]], lambda e: e.matmul(
                                banks[bs][:, :], lhsT=KT[:, kt * 128:(kt + 1) * 128], rhs=Qn[qs][:],
                                start=True, stop=False))
                            S.op("pe", [r_kpe, r_Qp[qs]], [rbank[bs]], lambda e: e.matmul(
                                banks[bs][:, :], lhsT=kpe2[hp:hp + 64, kt * 128:(kt + 1) * 128],
                                rhs=Qp[qs][hp:hp + 64, :], start=False, stop=True))
                            S.op("act", [rbank[bs]], [r_pT[kt % 3]], lambda e: e.activation(
                                out=pT[kt % 3][:], in_=banks[bs][:, :], func=AF.Exp, scale=SCALE))

                        def pv_mm(kt):
                            S.op("pe", [r_Vh, r_pT[kt % 3]], [rbank[5]], lambda e: e.matmul(
                                banks[5][:, :], lhsT=Vh[:, kt, :], rhs=pT[kt % 3][:],
                                start=(kt == 0), stop=(kt == nkt - 1)))
                            S.op("pe", [r_const, r_pT[kt % 3]], [rbank[6]], lambda e: e.matmul(
                                banks[6][:, :], lhsT=onesb[:], rhs=pT[kt % 3][:],
                                start=(kt == 0), stop=(kt == nkt - 1)))

                        s_mm(0)
                        if nkt > 1:
                            s_mm(1)
                        for kt in range(nkt):
                            pv_mm(kt)
                            if kt + 2 < nkt:
                                s_mm(kt + 2)
                        os_ = nq % 2
                        S.op("dve", [rbank[6]], [r_rden], lambda e: e.reciprocal(out=rden[:], in_=banks[6][:, :]))
                        S.op("dve", [rbank[5], r_rden], [r_ost[os_]], lambda e: e.tensor_tensor(
                            out=ost[os_][:], in0=banks[5][:, :], in1=rden[:], op=ALU.mult))
                        S.dma("pool", f"mo_st{os_}", oT_d[h, :, tq0:tq0 + 512], ost[os_][:], [r_ost[os_]],
                              [r_oT[tq0 // 512]])
            S.barrier()
        with contextlib.ExitStack() as ps:
            A = lambda name, shape, dt: sbx(ps, name, shape, dt)
            wo = A("wo", [128, 8, D], BF16); r_w = Reg()
            S.dma("sp", "mw_ld", wo[:], wo_b[j].rearrange("p (h n) -> p h n", h=8), [r_wo[j]], [r_w])
            xb = [A("xb", [128, NCH, TBK], F32) for _ in range(2)]; r_xb = [Reg() for _ in range(2)]
            ob = [A("ob", [128, 8, TBK], BF16) for _ in range(2)]; r_ob = [Reg() for _ in range(2)]
            for blk in range(NTOK // TBK):
                t0 = blk * TBK
                s2 = blk % 2
                S.dma("sp", f"oxb_ld{s2}", xb[s2][:], xT_v[:, :, t0:t0 + TBK], xT_regs(t0, TBK), [r_xb[s2]])
                S.dma("sp", f"oob_ld{s2}", ob[s2][:], oT_d[:, :, t0:t0 + TBK].rearrange("h p t -> p h t"),
                      [r_oT[blk]], [r_ob[s2]])
                for c in range(NCH):
                    bk = c % 4
                    for h in range(8):
                        S.op("pe", [r_w, r_ob[s2]], [rbank[bk]], lambda e, c=c, h=h, bk=bk: e.matmul(
                            banks[bk][:, :], lhsT=wo[:, h, c * 128:(c + 1) * 128], rhs=ob[s2][:, h, :],
                            start=(h == 0), stop=(h == 7)))
                    S.op("dve", [rbank[bk], r_xb[s2]], [r_xb[s2]], lambda e, c=c, bk=bk: e.tensor_tensor(
                        out=xb[s2][:, c, :], in0=xb[s2][:, c, :], in1=banks[bk][:, :], op=ALU.add))
                S.dma("pool", f"oxb_st{s2}", xT_v[:, :, t0:t0 + TBK], xb[s2][:], [r_xb[s2]], xT_regs(t0, TBK))
            S.barrier()


    def cc_allgather(src, dst, r_src, r_dst):
        S._deps("pool", [r_src], [r_dst])
        ins = nc.gpsimd.collective_compute("AllGather", ALU.bypass, replica_groups=[[0, 1, 2, 3], [4, 5, 6, 7]],
                                           ins=[src], outs=[dst])
        if "cc" not in S.dsem:
            S.dsem["cc"] = [S._newsem("d_cc"), 0]
        S.dsem["cc"][1] += 1
        ins.then_inc(S.dsem["cc"][0], 1)
        r_dst.w = ("d", "cc", S.dsem["cc"][1])
        r_dst.re = {}
        r_dst.rd = []
        r_src.rd.append(r_dst.w)

    def na_phase(layer):
        j = layer // 2
        _, nqkv_b, r_nqkv = W["nqkv"]
        _, nwo_b, r_nwo = W["nwo"]
        TBK = 512
        with contextlib.ExitStack() as ps:
            A = lambda name, shape, dt: sbx(ps, name, shape, dt)
            wqkv = A("wqkv", [128, NCH, 3072], BF16); r_w = Reg()
            for s6 in range(6):
                S.dma("sp", "nw_ld", wqkv[:, :, s6 * 512:(s6 + 1) * 512],
                      nqkv_b[j * 6 + s6].rearrange("p (c n) -> p c n", c=NCH), [r_nqkv[j * 6 + s6]], [r_w])
            bqk = A("bqk", [128, 16], F32)
            bvb = A("bvb", [128, D], F32)
            zt = A("zt", [128, 2048], BF16)
            S.dma("sp", "nw_ld", bqk[:], na_b[j], [], [r_w])
            S.dma("sp", "nw_ld", bvb[:], na_bv[j], [], [r_w])
            S.retoken([r_w], ("d", "nw_ld", S.dsem["nw_ld"][1]))
            r_zt = Reg()
            S.op("pool", [], [r_zt], lambda e: e.memset(zt[:], 0.0))
            for (lo, hi) in [(0, 256), (256 + TP, 512 + TP)]:
                S.dma("pool", "nz_st", nakE[0][:, :, lo:hi].rearrange("h p t -> p h t"),
                      zt[:].rearrange("p (h t) -> p h t", h=8), [r_zt], [r_nak[0]])
                S.dma("pool", "nz_st", navE[0][lo:hi, :].rearrange("(j p) d -> p j d", p=128),
                      zt[:].rearrange("p (j d) -> p j d", j=2), [r_zt], [r_nav[0]])
            xb = A("xb", [128, NCH, TBK], F32); r_xb = Reg()
            hb = A("hb", [128, NCH, TBK], BF16); r_hb = Reg()
            sq = A("sq", [128, NCH, TBK], BF16); r_sq = Reg()
            rs = A("rs", [128, TBK], F32); r_rs = Reg()
            qst = [A("qst", [128, 8, TBK], BF16) for _ in range(2)]; r_qst = [Reg() for _ in range(2)]
            kst = [A("kst", [128, 8, TBK], BF16) for _ in range(2)]; r_kst = [Reg() for _ in range(2)]
            vst = [A("vst", [128, 4, D], BF16) for _ in range(2)]; r_vst = [Reg() for _ in range(2)]
            for blk in range(NTOK // TBK):
                t0 = blk * TBK
                s2 = blk % 2
                seg = 0 if t0 < TP else 1
                lt0 = t0 - (0 if seg == 0 else TP)
                S.dma("sp", "nxb_ld", xb[:], xT_v[:, :, t0:t0 + TBK], xT_regs(t0, TBK), [r_xb])
                rmsnorm_fm(xb[:], [r_xb], NCH, TBK, lambda c: gsb[:, layer, c:c + 1], hb[:], r_hb,
                           sq[:], r_sq, rs[:], r_rs, 0)
                for kind in range(2):
                    st, r_st = (qst, r_qst) if kind == 0 else (kst, r_kst)
                    for hp in range(8):
                        bk = 1 + hp % 2
                        for c in range(NCH):
                            S.op("pe", [r_w, r_hb], [rbank[bk]], lambda e, c=c: e.matmul(
                                banks[bk][:, :], lhsT=wqkv[:, c, kind * 1024 + hp * 128:kind * 1024 + (hp + 1) * 128],
                                rhs=hb[:, c, :], start=(c == 0), stop=(c == NCH - 1)))
                        S.op("dve", [rbank[bk], r_w], [r_st[s2]], lambda e: e.tensor_scalar(
                            out=st[s2][:, hp, :], in0=banks[bk][:, :], scalar1=bqk[:, kind * 8 + hp:kind * 8 + hp + 1],
                            scalar2=(0.125 if kind == 0 else 1.0), op0=ALU.add, op1=ALU.mult))
                S.dma("pool", f"nq_st{s2}", naq[:, :, t0:t0 + TBK].rearrange("h p t -> p h t"), qst[s2][:],
                      [r_qst[s2]], [r_naq])
                S.dma("pool", f"nk_st{s2}", nakE[seg][:, :, 256 + lt0:256 + lt0 + TBK].rearrange("h p t -> p h t"),
                      kst[s2][:], [r_kst[s2]], [r_nak[seg]])
                for tl in range(4):
                    for half in range(2):
                        bk = 3 + half
                        for c in range(NCH):
                            S.op("pe", [r_w, r_hb], [rbank[bk]], lambda e, c=c: e.matmul(
                                banks[bk][:, :], lhsT=hb[:, c, tl * 128:(tl + 1) * 128],
                                rhs=wqkv[:, c, 2048 + half * 512:2048 + (half + 1) * 512],
                                start=(c == 0), stop=(c == NCH - 1)))
                        S.op("dve", [rbank[bk], r_w], [r_vst[s2]], lambda e: e.tensor_tensor(
                            out=vst[s2][:, tl, half * 512:(half + 1) * 512], in0=banks[bk][:, :],
                            in1=bvb[:, half * 512:(half + 1) * 512], op=ALU.add))
                S.dma("pool", f"nv_st{s2}", navE[seg][256 + lt0:256 + lt0 + TBK, :].rearrange("(j p) d -> p j d", p=128),
                      vst[s2][:], [r_vst[s2]], [r_nav[seg]])
            S.barrier()
        if cfg.get("NA_STOP", 9) <= 1:
            return
        with contextlib.ExitStack() as ps:
            A = lambda name, shape, dt: sbx(ps, name, shape, dt)
            mk = A("mk", [128, 8], F32); r_mk = Reg()
            S.dma("sp", "nh_ld", mk[:], na_MK[:, :], [], [r_mk])
            for g in range(2):
                sK = sendK_h[g].rearrange("(h p) t -> h p t", p=128)
                S.dma("sp", "nh_snd", sK[:, :, 0:256], nakE[1][g * 4:(g + 1) * 4, :, 256:512], [r_nak[1]], [r_send])
                S.dma("sp", "nh_snd", sK[:, :, 256:512], nakE[1][g * 4:(g + 1) * 4, :, TS:TS + 256], [r_nak[1]], [r_send])
            S.dma("sp", "nh_snd", sendV_h[0][:, :], navE[1][256:512, :], [r_nav[1]], [r_send])
            S.dma("sp", "nh_snd", sendV_h[1][:, :], navE[1][TS:TS + 256, :], [r_nav[1]], [r_send])
            S.retoken([r_send], ("d", "nh_snd", S.dsem["nh_snd"][1]))
            S.barrier()
            for g in range(2):
                cc_allgather(sendK_h[g][:, :], recvK_h[g][:, :], r_send, r_recv)
                cc_allgather(sendV_h[g][:, :], recvV_h[g][:, :], r_send, r_recv)
            S.barrier()
            rk = A("rk", [128, 4, 8, 512], BF16); r_rk = Reg()
            rv = A("rv", [128, 4, 4, D], BF16); r_rv = Reg()
            for g in range(2):
                rKv = recvK_h[g].rearrange("(r h p) t -> r p h t", r=4, p=128)
                rVv = recvV_h[g].rearrange("(r j p) d -> r p j d", r=4, p=128)
                for r in range(4):
                    S.dma("sp", "nh_ld", rk[:, r, g * 4:(g + 1) * 4, :], rKv[r], [r_recv], [r_rk])
                    S.dma("sp", "nh_ld", rv[:, r, g * 2:(g + 1) * 2, :], rVv[r], [r_recv], [r_rv])
            S.retoken([r_rk, r_rv, r_mk], ("d", "nh_ld", S.dsem["nh_ld"][1]))
            hk = A("hk", [128, 2, 8, 256], BF16); r_hk = Reg()
            hv = A("hv", [128, 2, 2, D], BF16); r_hv = Reg()
            for side in range(2):
                tsl = slice(256, 512) if side == 0 else slice(0, 256)
                jsl = slice(2, 4) if side == 0 else slice(0, 2)
                for r in range(4):
                    m = mk[:, side * 4 + r:side * 4 + r + 1]
                    if r == 0:
                        S.op("dve", [r_rk, r_mk], [r_hk], lambda e: e.tensor_scalar(
                            out=hk[:, side], in0=rk[:, r, :, tsl], scalar1=m, scalar2=None, op0=ALU.mult))
                        S.op("dve", [r_rv, r_mk], [r_hv], lambda e: e.tensor_scalar(
                            out=hv[:, side], in0=rv[:, r, jsl, :], scalar1=m, scalar2=None, op0=ALU.mult))
                    else:
                        S.op("dve", [r_rk, r_mk, r_hk], [r_hk], lambda e: e.scalar_tensor_tensor(
                            out=hk[:, side], in0=rk[:, r, :, tsl], scalar=m, in1=hk[:, side], op0=ALU.mult, op1=ALU.add))
                        S.op("dve", [r_rv, r_mk, r_hv], [r_hv], lambda e: e.scalar_tensor_tensor(
                            out=hv[:, side], in0=rv[:, r, jsl, :], scalar=m, in1=hv[:, side], op0=ALU.mult, op1=ALU.add))
                lo = 0 if side == 0 else 256 + TS
                S.dma("pool", "nh_st", nakE[1][:, :, lo:lo + 256].rearrange("h p t -> p h t"), hk[:, side],
                      [r_hk], [r_nak[1]])
                S.dma("pool", "nh_st", navE[1][lo:lo + 256, :].rearrange("(j p) d -> p j d", p=128), hv[:, side],
                      [r_hv], [r_nav[1]])
            S.barrier()
        if cfg.get("NA_STOP", 9) <= 2:
            return
        with contextlib.ExitStack() as ps:
            A = lambda name, shape, dt: sbx(ps, name, shape, dt)
            wo = A("nwo", [128, 8, D], BF16); r_w = Reg()
            S.dma("sp", "nw_ld", wo[:], nwo_b[j].rearrange("p (h n) -> p h n", h=8), [r_nwo[j]], [r_w])
            BI = A("BI", [128, 8, 512], F32)
            S.dma("sp", "nw_ld", BI[:], na_BI[j].rearrange("p (h n) -> p h n", h=8), [], [r_w])
            RM = A("RM", [128, 96], F32)
            S.dma("sp", "nw_ld", RM[:], na_RM[:, :], [], [r_w])
            S.retoken([r_w], ("d", "nw_ld", S.dsem["nw_ld"][1]))
            RB = min(16, RP, RS_)
            EB = RB + 8
            NBF = 1
            KTb = [A("KTb", [128, 8, EB * 64], BF16) for _ in range(NBF)]; r_KTb = [Reg() for _ in range(NBF)]
            QTb = [A("QTb", [128, 8, RB * 64], BF16) for _ in range(2)]; r_QTb = [Reg() for _ in range(2)]
            for hh in range(2):
                S.op("pool", [], [r_QTb[hh]], lambda e: e.memset(QTb[hh][:], 0.0))
            VE = [A("VE", [128, EB // 2, D], BF16) for _ in range(NBF)]; r_VE = [Reg() for _ in range(NBF)]
            VO = [A("VO", [128, EB // 2 - 1, D], BF16) for _ in range(NBF)]; r_VO = [Reg() for _ in range(NBF)]
            OTb = A("OTb", [128, 8, RB * 64], BF16); r_OTb = Reg()
            S.op("pool", [], [r_OTb], lambda e: e.memset(OTb[:], 0.0))
            xb = A("nxb", [128, NCH, RB * 64], F32); r_xb = Reg()
            EDt = [A("EDt", [128, 768], F32) for _ in range(2)]; r_EDt = [Reg() for _ in range(2)]
            sbias = [A("sbias", [128, 768], F32) for _ in range(2)]; r_sb = [Reg() for _ in range(2)]
            pT = [A("npT", [128, 6, 128], BF16) for _ in range(2)]; r_pT = [Reg() for _ in range(2)]
            rden = [A("nrden", [128, 128], F32) for _ in range(2)]; r_rden = [Reg() for _ in range(2)]
            unit = 0
            nblk_all = 0
            ned = 0
            for seg in range(2):
                R = RP if seg == 0 else RS_
                tok0 = 0 if seg == 0 else TP
                for rb in range(R // RB):
                    r_b = rb * RB
                    bs_ = nblk_all % NBF
                    nblk_all += 1
                    e0 = r_b * 64
                    S.dma("sp", f"nkt_ld{bs_}", KTb[bs_][:], nakE[seg][:, :, e0:e0 + EB * 64].rearrange("h p t -> p h t"),
                          [r_nak[seg]], [r_KTb[bs_]])
                    for hh in range(2):
                        S.dma("sp", f"nqt_ld{hh}", QTb[hh][hh * 64:(hh + 1) * 64],
                              naq[:, hh * 64:(hh + 1) * 64, tok0 + r_b * 64:tok0 + (r_b + RB) * 64].rearrange("h p t -> p h t"),
                              [r_naq], [r_QTb[hh]])
                    S.dma("sp", f"nve_ld{bs_}", VE[bs_][:],
                          navE[seg][e0:e0 + EB * 64, :].rearrange("(j p) d -> p j d", p=128), [r_nav[seg]], [r_VE[bs_]])
                    S.dma("sp", f"nvo_ld{bs_}", VO[bs_][:],
                          navE[seg][e0 + 64:e0 + 64 + (EB // 2 - 1) * 128, :].rearrange("(j p) d -> p j d", p=128),
                          [r_nav[seg]], [r_VO[bs_]])
                    S.dma("sp", "nxb2_ld", xb[:], xT_v[:, :, tok0 + r_b * 64:tok0 + (r_b + RB) * 64],
                          xT_regs(tok0 + r_b * 64, RB * 64), [r_xb])
                    for lr in range(RB):
                        r = r_b + lr
                        if cfg.get("NA_ATT", 2) == 0 or (cfg.get("NA_ATT", 2) == 1 and (r < 4 or r >= R - 4)):
                            continue
                        if r < 4:
                            cls, rr, nch, cstart = 1, r, 6, 0
                        elif r >= R - 4:
                            cls, rr, nch, cstart = 2, r - (R - 4), 6, (R - 4) - r_b
                        else:
                            cls, rr, nch, cstart = 0, 0, 4, lr
                        for hp in range(8):
                            u = unit % 2
                            unit += 1
                            bS = [u * 2, u * 2 + 1]
                            bO = 4 + u
                            bD = 6 + u
                            for i in range(nch):
                                for hh in range(2):
                                    col = (i % 4) * 128 + hh * 64
                                    bk = bS[i // 4]
                                    ks = (cstart + 2 * i) * 64
                                    S.op("pe", [r_KTb[bs_], r_QTb[hh]], [rbank[bk]], lambda e: e.matmul(
                                        banks[bk][:, col:col + 64], lhsT=KTb[bs_][:, hp, ks:ks + 128],
                                        rhs=QTb[hh][:, hp, lr * 64:(lr + 1) * 64],
                                        start=True, stop=True))
                            if cls == 0:
                                S.op("dve", [rbank[bS[0]], r_w], [r_sb[u]], lambda e: e.tensor_tensor(
                                    out=sbias[u][:, 0:512], in0=banks[bS[0]][:, :], in1=BI[:, hp, :], op=ALU.add))
                                S.op("act", [r_sb[u]], [r_pT[u]], lambda e: e.activation(
                                    out=pT[u][:, 0:4, :], in_=sbias[u][:, 0:512].rearrange("p (c n) -> p c n", c=4),
                                    func=AF.Exp))
                            else:
                                ed = ned % 2
                                ned += 1
                                S.dma("sp", f"ned_ld{ed}", EDt[ed][:], na_ED[j, rr, hp], [], [r_EDt[ed]])
                                for i in range(6):
                                    mcol = ((seg * 2 + (cls - 1)) * 4 + rr) * 6 + i
                                    bk = bS[i // 4]
                                    cs0 = (i % 4) * 128
                                    S.op("dve", [rbank[bk], r_EDt[ed], r_w], [r_sb[u]], lambda e: e.scalar_tensor_tensor(
                                        out=sbias[u][:, i * 128:(i + 1) * 128], in0=banks[bk][:, cs0:cs0 + 128],
                                        scalar=RM[:, mcol:mcol + 1], in1=EDt[ed][:, i * 128:(i + 1) * 128],
                                        op0=ALU.add, op1=ALU.add))
                                S.op("act", [r_sb[u]], [r_pT[u]], lambda e: e.activation(
                                    out=pT[u][:, :, :], in_=sbias[u][:, :].rearrange("p (c n) -> p c n", c=6),
                                    func=AF.Exp))
                            for i in range(nch):
                                ce = cstart + 2 * i
                                if ce % 2 == 0:
                                    vt, r_vt, vi = VE[bs_], r_VE[bs_], ce // 2
                                else:
                                    vt, r_vt, vi = VO[bs_], r_VO[bs_], (ce - 1) // 2
                                S.op("pe", [r_vt, r_pT[u]], [rbank[bO]], lambda e: e.matmul(
                                    banks[bO][:, 0:128], lhsT=vt[:, vi, hp * 128:(hp + 1) * 128], rhs=pT[u][:, i, :],
                                    start=(i == 0), stop=(i == nch - 1)))
                                S.op("pe", [r_const, r_pT[u]], [rbank[bD]], lambda e: e.matmul(
                                    banks[bD][:, 0:128], lhsT=onesb[:], rhs=pT[u][:, i, :],
                                    start=(i == 0), stop=(i == nch - 1)))
                            S.op("dve", [rbank[bD]], [r_rden[u]], lambda e: e.reciprocal(
                                out=rden[u][:], in_=banks[bD][:, 0:128]))
                            S.op("dve", [rbank[bO], r_rden[u]], [r_rden[u]], lambda e: e.tensor_tensor(
                                out=rden[u][:], in0=banks[bO][:, 0:128], in1=rden[u][:], op=ALU.mult))
                            for hh in range(2):
                                psl = slice(hh * 64, (hh + 1) * 64)
                                S.op("act", [r_rden[u]], [r_OTb], lambda e: e.copy(
                                    out=OTb[psl, hp, lr * 64:(lr + 1) * 64], in_=rden[u][psl, hh * 64:(hh + 1) * 64]))
                    for n0 in range(0, RB * 64, 512):
                        if cfg.get("NA_SKIPPROJ"):
                            break
                        nw = min(512, RB * 64 - n0)
                        for c in range(NCH):
                            bk = c % 2
                            for hp in range(8):
                                S.op("pe", [r_w, r_OTb], [rbank[bk]], lambda e: e.matmul(
                                    banks[bk][:, :nw], lhsT=wo[:, hp, c * 128:(c + 1) * 128], rhs=OTb[:, hp, n0:n0 + nw],
                                    start=(hp == 0), stop=(hp == 7)))
                            S.op("dve", [rbank[bk], r_xb], [r_xb], lambda e: e.tensor_tensor(
                                out=xb[:, c, n0:n0 + nw], in0=xb[:, c, n0:n0 + nw], in1=banks[bk][:, :nw], op=ALU.add))
                    S.dma("pool", "nxb_st", xT_v[:, :, tok0 + r_b * 64:tok0 + (r_b + RB) * 64], xb[:], [r_xb],
                          xT_regs(tok0 + r_b * 64, RB * 64))
            S.barrier()

    for layer in LAYERS:
        if DO_MIX and layer % 2 == 0:
            mla_phase(layer)
        if DO_MIX and layer % 2 == 1:
            na_phase(layer)
        if DO_PEER:
            peer_phase(layer)

    with contextlib.ExitStack() as ps:
        xtb = [sbx(ps, "xtb", [128, NCH, 512], F32) for i in range(2)]
        r_xtb = [Reg() for i in range(2)]
        sqb = sbx(ps, "sqb", [128, NCH, 512], BF16)
        r_sqb = Reg("sqb")
        rsb = sbx(ps, "rsb", [128, 512], F32)
        r_rsb = Reg("rsb")
        ynb = [sbx(ps, "ynb", [128, NCH, 512], F32) for i in range(2)]
        r_ynb = [Reg() for i in range(2)]
        yt = [sbx(ps, "yt", [128, D], F32) for i in range(2)]
        r_yt = [Reg() for i in range(2)]
        r_y = [Reg() for i in range(NTOK // 128)]
        nyt = 0
        for b in range(NB):
            s = b % 2
            S.dma("sp", f"xtbld{s}", xtb[s][:], xT_v[:, :, b * 512:(b + 1) * 512], xT_regs(b * 512, 512), [r_xtb[s]])
            rmsnorm_fm(xtb[s][:], [r_xtb[s]], NCH, 512, lambda c: gsb[:, 8, c:c + 1], ynb[s][:], r_ynb[s],
                       sqb[:], r_sqb, rsb[:], r_rsb, 4)
            for j in range(4):
                ys = nyt % 2
                nyt += 1
                for half in range(2):
                    bk = (2 * j + half) % 4
                    for cc in range(4):
                        c = half * 4 + cc
                        S.op("pe", [r_ynb[s], r_const], [rbank[bk]],
                             lambda e, c=c, j=j, cc=cc, bk=bk, s=s: e.transpose(
                                 out=banks[bk][:, cc * 128:(cc + 1) * 128], in_=ynb[s][:, c, j * 128:(j + 1) * 128],
                                 identity=identf[:]))
                    evac_copy(banks[bk][:, :], yt[ys][:, half * 512:(half + 1) * 512], [rbank[bk]], [r_yt[ys]])
                ti = b * 4 + j
                S.dma("pool", f"ytst{ys}", y_out[ti * 128:(ti + 1) * 128, :], yt[ys][:], [r_yt[ys]], [r_y[ti]])
        S.finish(r_y + [r_dbg])
    K.es = es
    K.dbg = list(dbg.keys())
    K.stats = (S.nsem, dict(S.cnt))
    return nc


def host_consts(inp, cfg):
    DEPTH = cfg["DEPTH"]
    g = np.concatenate([inp["norm_mix"], inp["norm_ffn"], inp["norm_final"][None]], axis=0)
    gains = np.ascontiguousarray(g.reshape(9, NCH, 128).transpose(2, 0, 1)).astype(np.float32)
    out = {"ident_f": np.eye(128, dtype=np.float32), "gains": gains,
           "iota_in": np.ascontiguousarray(np.broadcast_to(np.arange(128, dtype=np.float32), (128, 128)))}
    if cfg.get("DO_PEER", True):
        u = inp["peer_u"]
        L = u.shape[0]
        pu = u.reshape(L, 32, 512, NCH, 128).transpose(0, 1, 4, 3, 2).reshape(L * 32, 128, 4096)
        out["peer_u"] = np.ascontiguousarray(pu)
        v = inp["peer_v"]
        pv = v.reshape(L, 32, 4, 128, D).transpose(0, 1, 3, 2, 4).reshape(L * 32, 128, 4096)
        out["peer_v"] = np.ascontiguousarray(pv)
        wq = inp["peer_w_q"]
        pwq = wq.reshape(L, NCH, 128, 4, 512).transpose(0, 3, 2, 1, 4).reshape(L * 4, 128, 4096)
        out["peer_wq"] = np.ascontiguousarray(pwq)
        sk = inp["peer_sub_keys"]
        psk = sk.reshape(L, 16, 128, 128).transpose(0, 3, 1, 2).reshape(L, 128, 2048)
        out["peer_sk"] = np.ascontiguousarray(psk)
    if cfg.get("DO_MIX", True) and "mla_w_dq" in inp:
        Lm = inp["mla_w_dq"].shape[0]
        wdq = inp["mla_w_dq"]
        out["mla_wdq"] = np.ascontiguousarray(wdq.reshape(Lm, NCH, 128, 384).transpose(0, 2, 1, 3).reshape(Lm, 128, 3072))
        wuq = inp["mla_w_uq"]
        cols = []
        for h in range(8):
            cols += [h * 192 + d for d in range(128)]
        for h in range(8):
            cols += [h * 192 + 128 + d for d in range(64)]
        for h in range(8):
            cols += [h * 192 + 128 + (d + 32) % 64 for d in range(64)]
        wuq2 = wuq[:, :, cols]
        out["mla_wuq"] = np.ascontiguousarray(wuq2.reshape(Lm, 3, 128, 2048).transpose(0, 2, 1, 3).reshape(Lm, 128, 6144))
        wdkv = inp["mla_w_dkv"]
        cols = list(range(256)) + [256 + p % 64 for p in range(128)] + [256 + (p % 64 + 32) % 64 for p in range(128)]
        wdkv2 = wdkv[:, :, cols]
        out["mla_wdkv"] = np.ascontiguousarray(wdkv2.reshape(Lm, NCH, 128, 512).transpose(0, 2, 1, 3).reshape(Lm, 128, 4096))
        wukv = inp["mla_w_ukv"]
        out["mla_wukv"] = np.ascontiguousarray(wukv.reshape(Lm, 2, 128, 2048).transpose(0, 2, 1, 3).reshape(Lm, 128, 4096))
        wo = inp["mla_w_o"]
        out["mla_wo"] = np.ascontiguousarray(wo.reshape(Lm, 8, 128, D).transpose(0, 2, 1, 3).reshape(Lm, 128, 8192))
        g = np.concatenate([inp["mla_q_norm"].reshape(Lm, 3, 128), inp["mla_kv_norm"].reshape(Lm, 2, 128)], axis=1)
        out["mla_g"] = np.ascontiguousarray(g.transpose(0, 2, 1)).astype(np.float32)
    if cfg.get("DO_MIX", True) and "na_w_qkv" in inp:
        w = inp["na_w_qkv"]
        Ln = w.shape[0]
        out["na_wqkv"] = np.ascontiguousarray(
            w.reshape(Ln, NCH, 128, 6, 512).transpose(0, 3, 2, 1, 4).reshape(Ln * 6, 128, 4096))
        wo = inp["na_w_o"]
        out["na_wo"] = np.ascontiguousarray(wo.reshape(Ln, 8, 128, D).transpose(0, 2, 1, 3).reshape(Ln, 128, 8192))
        b = inp["na_b_qkv"]
        bq = b[:, 0:1024].reshape(Ln, 8, 128).transpose(0, 2, 1)
        bk = b[:, 1024:2048].reshape(Ln, 8, 128).transpose(0, 2, 1)
        out["na_b"] = np.ascontiguousarray(np.concatenate([bq, bk], axis=2)).astype(np.float32)
        out["na_bv"] = np.ascontiguousarray(np.broadcast_to(b[:, None, 2048:3072], (Ln, 128, D))).astype(np.float32)
        rpb = inp["na_rpb"]
        kp = np.arange(128)
        krl, kc = kp // 64, kp % 64
        cq = np.arange(64)
        c0 = np.clip(cq - 8, 0, 48)
        colok = (kc[:, None] >= c0[None, :]) & (kc[:, None] < c0[None, :] + 16)
        relc = np.clip(kc[:, None] - cq[None, :] + 15, 0, 30)

        def table(dr_of_chunk, nchunk):
            t = np.full((Ln, 128, 16, nchunk, 64), -30000.0, np.float32)
            for i in range(nchunk):
                dr = dr_of_chunk(i) + krl
                ok = (dr >= -7) & (dr <= 7)
                drc = np.clip(dr + 7, 0, 14)
                vals = rpb[:, :, drc[:, None], relc]
                m = (ok[:, None] & colok)[None, None]
                t[:, :, :, i, :] = np.where(m, vals, np.float32(-30000.0)).transpose(0, 2, 1, 3)
            t = t.reshape(Ln, 128, 8, 2, nchunk, 64).transpose(0, 1, 2, 4, 3, 5)
            return np.ascontiguousarray(t)

        out["na_BI"] = table(lambda i: -4 + 2 * i, 4).reshape(Ln, 128, 4096)
        ed = np.stack([table(lambda i, rr=rr: -4 + 2 * i - rr, 6) for rr in range(4)], axis=1)
        out["na_ED"] = np.ascontiguousarray(ed.transpose(0, 1, 3, 2, 4, 5, 6).reshape(Ln, 4, 8, 128, 768))
    return out


def na_percore(cfg, c):
    TP, TS = cfg["TP"], cfg["TS"]
    RP, RS = TP // 64, TS // 64
    qi = c % 4
    rm = np.zeros((128, 96), np.float32)
    krl = np.arange(128) // 64
    for seg in range(2):
        R = RP if seg == 0 else RS
        base = 0 if seg == 0 else qi * RS
        Rg = RP if seg == 0 else 4 * RS
        for edge in range(2):
            for rr in range(4):
                ql = rr if edge == 0 else R - 4 + rr
                rg = base + ql
                r0 = min(max(rg - 4, 0), Rg - 8)
                for i in range(6):
                    kl = (-4 + 2 * i if edge == 0 else R - 8 + 2 * i) + krl
                    kg = base + kl
                    ok = (kg >= r0) & (kg < r0 + 8)
                    rm[:, ((seg * 2 + edge) * 4 + rr) * 6 + i] = np.where(ok, 0.0, -30000.0)
    mk = np.zeros((128, 8), np.float32)
    if qi > 0:
        mk[:, qi - 1] = 1.0
    if qi < 3:
        mk[:, 4 + qi + 1] = 1.0
    return {"na_RM": rm, "na_MK": mk}


def rope_tables(positions):
    f = (np.arange(128) % 64) % 32
    inv = (np.float32(10000.0) ** (-(np.arange(0, 64, 2, dtype=np.float32)) / np.float32(64))).astype(np.float32)
    ang = (positions.astype(np.float32)[None, :] * inv[f][:, None]).astype(np.float32)
    sign = np.where((np.arange(128) % 64) < 32, -1.0, 1.0).astype(np.float32)
    return np.cos(ang).astype(np.float32), (np.sin(ang) * sign[:, None]).astype(np.float32)


def run(cfg, inp, x_prompt, x_sample):
    TP, TS = cfg["TP"], cfg["TS"]
    nc = build(cfg)
    consts = host_consts(inp, cfg)
    in_maps = []
    for c in range(NCORES):
        xs = x_sample[c // 4, (c % 4) * TS:(c % 4 + 1) * TS]
        m = dict(consts)
        m["x_in"] = np.ascontiguousarray(np.concatenate([x_prompt[c], xs], axis=0))
        if "mla_wdq" in consts:
            pos = np.concatenate([np.arange(TP), (c % 4) * TS + np.arange(TS)])
            m["rope_c"], m["rope_s"] = rope_tables(pos)
        if "na_wqkv" in consts:
            m.update(na_percore(cfg, c))
        in_maps.append(m)
    res = run_bass_kernel_spmd(nc, in_maps, core_ids=list(range(NCORES)))
    K.es.close()
    K.res = res
    yp = np.stack([res.results[c]["y_out"][:TP] for c in range(NCORES)], axis=0)
    ys = np.stack([np.concatenate([res.results[g * 4 + q]["y_out"][TP:] for q in range(4)], axis=0)
                   for g in range(2)], axis=0)
    return yp.astype(np.float32), ys.astype(np.float32)


def kernel(**inputs):
    cfg = dict(TP=2048, TS=4096, DEPTH=4)
    inp = {k: np.asarray(v) for k, v in inputs.items()}
    return run(cfg, inp, inp["x_prompt"], inp["x_sample"])
```
